# Optimizing a Trainium2 kernel written in Bass

```python
import jax, jax.numpy as jnp
from jax import lax
import numpy as np

D_MODEL = 1024
BATCH = 16
SEQ = 256
DEPTH = 2
DEC_BATCH = 4
DEC_SEQ = 2048
PAST_LEN = 512

GRID_W = 64
ROPE_THETA = 10000.0
EPS = 1e-6
Q_BLOCK = 128
N_MOD = 6

HEADS_A = 8
KV_HEADS_A = 2
HD_A = 64
GROUP_A = HEADS_A // KV_HEADS_A
HEADS_B = 8
Q_LORA = 384
KV_LORA = 256
NOPE_B = 64
ROPE_B = 32
VD_B = 64
HEADS_C = 8
DK_C = 64
DV_C = 64
CONV_K = 3
CHUNK = 64
CONV_CH = 2 * HEADS_C * DK_C + HEADS_C * DV_C
D_FF = 256 * ((8 * D_MODEL + 3 * 256 - 1) // (3 * 256))

SPLIT_SIZES = (
    HEADS_A * HD_A, KV_HEADS_A * HD_A, KV_HEADS_A * HD_A,
    Q_LORA, KV_LORA, ROPE_B,
    CONV_CH, HEADS_C * DV_C, 2 * HEADS_C, 2 * HEADS_C,
    3 * D_MODEL,
)
IN_COLS = sum(SPLIT_SIZES)

kernel_name = 'hybrid_prefix_diffusion_trunk'


def split_cols(x, sizes):
    offs = np.cumsum(sizes)[:-1].tolist()
    return jnp.split(x, offs, axis=-1)


def rms_norm(x, g):
    xf = x.astype(jnp.float32)
    y = xf * lax.rsqrt(jnp.mean(xf * xf, axis=-1, keepdims=True) + EPS)
    return (y * g.astype(jnp.float32)).astype(x.dtype)


def l2_norm(x):
    xf = x.astype(jnp.float32)
    return xf * lax.rsqrt(jnp.sum(xf * xf, axis=-1, keepdims=True) + EPS)


def axial_angles(n, rot_dim):
    rows = n // GRID_W
    row = jnp.repeat(jnp.arange(rows, dtype=jnp.float32), GRID_W)
    col = jnp.tile(jnp.arange(GRID_W, dtype=jnp.float32), rows)
    nf = rot_dim // 4
    inv_freq = ROPE_THETA ** (-jnp.arange(nf, dtype=jnp.float32) / nf)
    return row[:, None] * inv_freq, col[:, None] * inv_freq


def rotate(x, ang):
    x1, x2 = jnp.split(x, 2, axis=-1)
    cos = jnp.cos(ang)[None, :, None, :].astype(x.dtype)
    sin = jnp.sin(ang)[None, :, None, :].astype(x.dtype)
    return jnp.concatenate([x1 * cos - x2 * sin, x2 * cos + x1 * sin], axis=-1)


def axial_rope(x, angs):
    xr, xc = jnp.split(x, 2, axis=-1)
    return jnp.concatenate([rotate(xr, angs[0]), rotate(xc, angs[1])], axis=-1)


def block_attention(q, k, v, scale):
    b, s = q.shape[:2]
    nblk = s // Q_BLOCK
    qb = jnp.swapaxes(q.reshape((b, nblk, Q_BLOCK) + q.shape[2:]), 0, 1)

    def one_block(qi):
        sc = jnp.einsum('bqkgd,btkd->bkgqt', qi, k).astype(jnp.float32) * scale
        p = jax.nn.softmax(sc, axis=-1).astype(v.dtype)
        return jnp.einsum('bkgqt,btkd->bqkgd', p, v)

    out = lax.map(one_block, qb)
    return jnp.swapaxes(out, 0, 1).reshape((b, s) + out.shape[3:])


def attn_gqa(qa, ka, va, p, ctx, angs):
    b, n = qa.shape[:2]
    q = rms_norm(qa.reshape(b, n, HEADS_A, HD_A), p['a_qnorm'])
    k = rms_norm(ka.reshape(b, n, KV_HEADS_A, HD_A), p['a_knorm'])
    v = va.reshape(b, n, KV_HEADS_A, HD_A)
    if ctx is None:
        kk, vv, new = k, v, (k, v)
    else:
        q = axial_rope(q, angs)
        k = axial_rope(k, angs)
        kk = jnp.concatenate([ctx[0].astype(k.dtype), k], axis=1)
        vv = jnp.concatenate([ctx[1].astype(v.dtype), v], axis=1)
        new = None
    o = block_attention(q.reshape(b, n, KV_HEADS_A, GROUP_A, HD_A), kk, vv, HD_A ** -0.5)
    return o.reshape(b, n, HEADS_A * HD_A), new


def attn_mla(cq, ckv, kpe, p, ctx, angs):
    b, n = cq.shape[:2]
    q = (rms_norm(cq, p['b_qnorm']) @ p['w_uq']).reshape(b, n, HEADS_B, NOPE_B + ROPE_B)
    q_nope, q_pe = q[..., :NOPE_B], q[..., NOPE_B:]
    ckv = rms_norm(ckv, p['b_kvnorm'])
    if ctx is None:
        ckv_all, kpe_all, new = ckv, kpe, (ckv, kpe)
    else:
        q_pe = axial_rope(q_pe, angs)
        kpe = axial_rope(kpe[:, :, None, :], angs)[:, :, 0, :]
        ckv_all = jnp.concatenate([ctx[0].astype(ckv.dtype), ckv], axis=1)
        kpe_all = jnp.concatenate([ctx[1].astype(kpe.dtype), kpe], axis=1)
        new = None
    t = ckv_all.shape[1]
    kv = (ckv_all @ p['w_ukv']).reshape(b, t, HEADS_B, NOPE_B + VD_B)
    k = jnp.concatenate([kv[..., :NOPE_B],
                         jnp.broadcast_to(kpe_all[:, :, None, :], (b, t, HEADS_B, ROPE_B)).astype(kv.dtype)], axis=-1)
    qf = jnp.concatenate([q_nope, q_pe], axis=-1)[:, :, :, None, :]
    o = block_attention(qf, k, kv[..., NOPE_B:], (NOPE_B + ROPE_B) ** -0.5)
    return o.reshape(b, n, HEADS_B * VD_B), new


def chunk_gated_delta(q, k, v, g, beta, s0):
    b, n, h, _ = q.shape
    dv = v.shape[-1]
    nc = n // CHUNK

    def to_chunks(t):
        return jnp.transpose(t.reshape(b, nc, CHUNK, h, t.shape[-1]), (1, 0, 3, 2, 4))

    qc, kc, vc = to_chunks(q), to_chunks(k), to_chunks(v)
    gc = jnp.cumsum(to_chunks(g[..., None])[..., 0], axis=-1)
    bc = to_chunks(beta[..., None])[..., 0]
    idx = jnp.arange(CHUNK)
    lower = idx[:, None] >= idx[None, :]
    strict = idx[:, None] > idx[None, :]
    diff = gc[..., :, None] - gc[..., None, :]
    decay = jnp.where(lower, jnp.exp(jnp.where(lower, diff, 0.0)), 0.0)
    kb = kc * bc[..., None]
    a_mat = jnp.where(strict, jnp.einsum('nbhid,nbhjd->nbhij', kb, kc) * decay, 0.0)
    t_mat = a_mat + jnp.eye(CHUNK, dtype=jnp.float32)
    rhs = jnp.concatenate([vc * bc[..., None], kb * jnp.exp(gc)[..., None]], axis=-1)
    sol = lax.linalg.triangular_solve(t_mat, rhs, left_side=True, lower=True, unit_diagonal=True)
    u, w = sol[..., :dv], sol[..., dv:]
    qk = jnp.einsum('nbhid,nbhjd->nbhij', qc, kc) * decay

    def step(s, xs):
        q_i, k_i, u_i, w_i, g_i, qk_i = xs
        v_new = u_i - jnp.einsum('bhck,bhkv->bhcv', w_i, s)
        o = (jnp.einsum('bhck,bhkv->bhcv', q_i * jnp.exp(g_i)[..., None], s)
             + jnp.einsum('bhij,bhjv->bhiv', qk_i, v_new))
        g_last = g_i[..., -1:]
        s = (s * jnp.exp(g_last)[..., None]
             + jnp.einsum('bhck,bhcv->bhkv', k_i * jnp.exp(g_last - g_i)[..., None], v_new))
        return s, o

    s_fin, o = lax.scan(step, s0, (qc, kc, u, w, gc, qk))
    o = jnp.transpose(o, (1, 0, 3, 2, 4)).reshape(b, n, h, dv)
    return o, s_fin


def gated_deltanet(qkv, z, a_in, b_in, p, ctx):
    b, n = qkv.shape[:2]
    w = p['c_conv'].astype(qkv.dtype)
    qkv = jax.nn.silu(lax.conv_general_dilated(
        qkv, w, window_strides=(1,), padding=[(CONV_K // 2, CONV_K // 2)],
        dimension_numbers=('NWC', 'WIO', 'NWC'), feature_group_count=CONV_CH))
    q, k, v = split_cols(qkv, (HEADS_C * DK_C, HEADS_C * DK_C, HEADS_C * DV_C))
    q = l2_norm(q.reshape(b, n, HEADS_C, DK_C)) * (DK_C ** -0.5)
    k = l2_norm(k.reshape(b, n, HEADS_C, DK_C))
    v = v.reshape(b, n, HEADS_C, DV_C).astype(jnp.float32)
    a = a_in.reshape(b, n, 2, HEADS_C).astype(jnp.float32)
    g = -jnp.exp(p['c_alog'].astype(jnp.float32)) * jax.nn.softplus(a + p['c_dt_bias'].astype(jnp.float32))
    beta = jax.nn.sigmoid(b_in.reshape(b, n, 2, HEADS_C).astype(jnp.float32))
    if ctx is None:
        s0_f = jnp.zeros((b, HEADS_C, DK_C, DV_C), jnp.float32)
        s0_b = s0_f
    else:
        s0_f, s0_b = ctx[0].astype(jnp.float32), ctx[1].astype(jnp.float32)
    o_f, s_f = chunk_gated_delta(q, k, v, g[:, :, 0], beta[:, :, 0], s0_f)
    flip = lambda t: jnp.flip(t, axis=1)
    o_b, s_b = chunk_gated_delta(flip(q), flip(k), flip(v), flip(g[:, :, 1]), flip(beta[:, :, 1]), s0_b)
    o = o_f + flip(o_b)
    o = rms_norm(o, p['c_onorm']).astype(z.dtype) * jax.nn.silu(z.reshape(b, n, HEADS_C, DV_C))
    return o.reshape(b, n, HEADS_C * DV_C), (s_f, s_b)


def mixer_block(h, p, ctx, angs):
    (qa, ka, va, cq, ckv, kpe, qkv, z, a_in, b_in, gates) = split_cols(h @ p['w_in'], SPLIT_SIZES)
    ctx_a = None if ctx is None else ctx[0:2]
    ctx_b = None if ctx is None else ctx[2:4]
    ctx_c = None if ctx is None else ctx[4:6]
    ya, new_a = attn_gqa(qa, ka, va, p, ctx_a, None if angs is None else angs['a'])
    yb, new_b = attn_mla(cq, ckv, kpe, p, ctx_b, None if angs is None else angs['b'])
    yc, new_c = gated_deltanet(qkv, z, a_in, b_in, p, ctx_c)
    g_a, g_b, g_c = jnp.split(jax.nn.sigmoid(gates), 3, axis=-1)
    merged = g_a * (ya @ p['w_pa']) + g_b * (yb @ p['w_pb']) + g_c * (yc @ p['w_pc'])
    new_ctx = (new_a + new_b + new_c) if ctx is None else None
    return merged @ p['w_out'], new_ctx


def trunk_layer(x, cvec, p, ctx, angs):
    mod = (jax.nn.silu(cvec) @ p['w_mod'] + p['b_mod'])[:, None, :]
    sh1, sc1, g1, sh2, sc2, g2 = jnp.split(mod, N_MOD, axis=-1)
    h = rms_norm(x, p['norm1']) * (1 + sc1) + sh1
    mix, new_ctx = mixer_block(h, p, ctx, angs)
    x = x + g1 * mix
    h = rms_norm(x, p['norm2']) * (1 + sc2) + sh2
    x = x + g2 * ((jax.nn.silu(h @ p['w_gate']) * (h @ p['w_up'])) @ p['w_down'])
    return x, new_ctx


def setup_inputs(seed: int = 0) -> dict:
    key = jax.random.key(seed)
    ks = iter(jax.random.split(key, 48))

    def nrm(shape, scale=1.0):
        return jax.random.normal(next(ks), shape, jnp.float32) * scale

    def gain(shape):
        return 1.0 + 0.05 * jax.random.normal(next(ks), shape, jnp.float32)

    L, D = DEPTH, D_MODEL
    dt = jnp.exp(jax.random.uniform(next(ks), (L, 2, HEADS_C), jnp.float32, np.log(1e-3), np.log(1e-1)))
    return {
        'x_prompt': nrm((BATCH, SEQ, D)),
        'x_sample': nrm((DEC_BATCH, DEC_SEQ, D)),
        'cache_ka': nrm((DEC_BATCH, L, PAST_LEN, KV_HEADS_A, HD_A)),
        'cache_va': nrm((DEC_BATCH, L, PAST_LEN, KV_HEADS_A, HD_A)),
        'cache_ckv': nrm((DEC_BATCH, L, PAST_LEN, KV_LORA)),
        'cache_kpe': nrm((DEC_BATCH, L, PAST_LEN, ROPE_B)),
        'state_fwd': nrm((DEC_BATCH, L, HEADS_C, DK_C, DV_C), 0.1),
        'state_bwd': nrm((DEC_BATCH, L, HEADS_C, DK_C, DV_C), 0.1),
        'c': nrm((DEC_BATCH, D)),
        'c_ctx': nrm((D,)),
        'w_mod': nrm((L, D, N_MOD * D), D ** -0.5),
        'b_mod': nrm((L, N_MOD * D), 0.02),
        'norm1': gain((L, D)),
        'norm2': gain((L, D)),
        'w_in': nrm((L, D, IN_COLS), D ** -0.5),
        'a_qnorm': gain((L, HD_A)),
        'a_knorm': gain((L, HD_A)),
        'b_qnorm': gain((L, Q_LORA)),
        'b_kvnorm': gain((L, KV_LORA)),
        'w_uq': nrm((L, Q_LORA, HEADS_B * (NOPE_B + ROPE_B)), Q_LORA ** -0.5),
        'w_ukv': nrm((L, KV_LORA, HEADS_B * (NOPE_B + VD_B)), KV_LORA ** -0.5),
        'c_conv': nrm((L, CONV_K, 1, CONV_CH), CONV_K ** -0.5),
        'c_alog': jnp.log(jax.random.uniform(next(ks), (L, 2, HEADS_C), jnp.float32, 1.0, 16.0)),
        'c_dt_bias': jnp.log(jnp.expm1(dt)),
        'c_onorm': gain((L, DV_C)),
        'w_pa': nrm((L, HEADS_A * HD_A, D), (HEADS_A * HD_A) ** -0.5),
        'w_pb': nrm((L, HEADS_B * VD_B, D), (HEADS_B * VD_B) ** -0.5),
        'w_pc': nrm((L, HEADS_C * DV_C, D), (HEADS_C * DV_C) ** -0.5),
        'w_out': nrm((L, D, D), D ** -0.5),
        'w_gate': nrm((L, D, D_FF), D ** -0.5),
        'w_up': nrm((L, D, D_FF), D ** -0.5),
        'w_down': nrm((L, D_FF, D), D_FF ** -0.5),
        'final_norm': gain((D,)),
    }


def reference(x_prompt, x_sample, cache_ka, cache_va, cache_ckv, cache_kpe, state_fwd, state_bwd,
              c, c_ctx, w_mod, b_mod, norm1, norm2, w_in, a_qnorm, a_knorm, b_qnorm, b_kvnorm,
              w_uq, w_ukv, c_conv, c_alog, c_dt_bias, c_onorm, w_pa, w_pb, w_pc, w_out,
              w_gate, w_up, w_down, final_norm):
    def layer_params(l):
        return {'w_mod': w_mod[l], 'b_mod': b_mod[l], 'norm1': norm1[l], 'norm2': norm2[l],
                'w_in': w_in[l], 'a_qnorm': a_qnorm[l], 'a_knorm': a_knorm[l],
                'b_qnorm': b_qnorm[l], 'b_kvnorm': b_kvnorm[l], 'w_uq': w_uq[l], 'w_ukv': w_ukv[l],
                'c_conv': c_conv[l], 'c_alog': c_alog[l], 'c_dt_bias': c_dt_bias[l], 'c_onorm': c_onorm[l],
                'w_pa': w_pa[l], 'w_pb': w_pb[l], 'w_pc': w_pc[l], 'w_out': w_out[l],
                'w_gate': w_gate[l], 'w_up': w_up[l], 'w_down': w_down[l]}

    x = x_prompt
    outs = []
    for l in range(DEPTH):
        x, ctx_l = trunk_layer(x, c_ctx[None, :], layer_params(l), None, None)
        outs.append(ctx_l)
    y_prompt = rms_norm(x, final_norm)
    new_ka, new_va, new_ckv, new_kpe, new_sf, new_sb = [
        jnp.stack([o[i] for o in outs], axis=1) for i in range(6)]

    n_lat = x_sample.shape[1]
    angs = {'a': axial_angles(n_lat, HD_A), 'b': axial_angles(n_lat, ROPE_B)}
    x = x_sample
    for l in range(DEPTH):
        ctx = (cache_ka[:, l], cache_va[:, l], cache_ckv[:, l], cache_kpe[:, l],
               state_fwd[:, l], state_bwd[:, l])
        x, _ = trunk_layer(x, c, layer_params(l), ctx, angs)
    y_sample = rms_norm(x, final_norm)

    return (y_prompt, y_sample, new_ka, new_va, new_ckv, new_kpe, new_sf, new_sb)
```

```python
import numpy as np
import concourse.bass as bass
import concourse.mybir as mybir
from contextlib import ExitStack

F32 = mybir.dt.float32
BF16 = mybir.dt.bfloat16
F32R = mybir.dt.float32r
AF = mybir.ActivationFunctionType
ALU = mybir.AluOpType
AX = mybir.AxisListType


class Tile:
    def __init__(self, name, t):
        self.name = name
        self.t = t
        self.w = None
        self.r = {}

    def __getitem__(self, idx):
        return V(self, self.t[idx])

    def ap(self):
        return V(self, self.t[:])


class V:
    def __init__(self, tile, ap):
        self.tile = tile
        self.a = ap

    def __getitem__(self, idx):
        return V(self.tile, self.a[idx])

    def re(self, pat, **kw):
        return V(self.tile, self.a.rearrange(pat, **kw))

    def bc(self, shape):
        return V(self.tile, self.a.to_broadcast(shape))

    def bitcast(self, dt):
        return V(self.tile, self.a.bitcast(dt))


def _ap(x):
    return x.a if isinstance(x, V) else x


class KB:
    EPOCH = 30000
    NDSEM = 16
    SAME_ENGINE_SYNC = True

    def __init__(self, nc, st, gst=None):
        self.nc = nc
        self.st = st
        self.gst = gst if gst is not None else st
        self.eng = dict(pe=nc.tensor, act=nc.scalar, dve=nc.vector, pool=nc.gpsimd, sp=nc.sync)
        self.esem = {}
        self.ecnt = {}
        self.nsem = 0
        for e in self.eng:
            self.esem[e] = self._new_sem()
            self.ecnt[e] = 0
        self.waited = {}
        self.dsem = [self._new_sem() for _ in range(self.NDSEM)]
        self.dval = [0] * self.NDSEM
        self.di = 0
        self.dpi = 0
        self.ninst = 0
        self.nwait = 0
        self.out_tokens = []
        self.psum_banks = []
        self.psum_i = 0

    def _new_sem(self):
        self.nsem += 1
        return self.gst.enter_context(self.nc.semaphore("s%d" % self.nsem))

    def sb(self, name, shape, dt):
        self.nalloc = getattr(self, "nalloc", 0) + 1
        base = name
        name = "%s_%d" % (name, self.nalloc)
        if not hasattr(self, "names"):
            self.names = {}
        self.names[base] = name
        t = self.st.enter_context(self.nc.sbuf_tensor(name, list(shape), dt))
        return Tile(name, t)

    def ps(self, name, shape, dt):
        t = self.st.enter_context(self.nc.psum_tensor(name, list(shape), dt))
        return Tile(name, t)

    def dram(self, name, shape, dt, kind="Internal"):
        t = self.nc.dram_tensor(name, list(shape), dt, kind=kind)
        return Tile(name, t)

    def _wait(self, e, tok):
        if tok is None:
            return
        sem, val = tok
        if (not self.SAME_ENGINE_SYNC or e == "pe") and sem is self.esem[e]:
            return
        key = (e, sem.num)
        if self.waited.get(key, 0) >= val:
            return
        self.eng[e].wait_ge(sem, val)
        self.waited[key] = val
        self.nwait += 1

    def _deps(self, e, reads, writes):
        for t in reads:
            self._wait(e, t.w)
        for t in writes:
            self._wait(e, t.w)
            for tok in t.r.values():
                self._wait(e, tok)

    def _mark(self, key, tok, reads, writes):
        for t in reads:
            t.r[key] = tok
        for t in writes:
            t.w = tok
            t.r = {}

    mute = False

    def op(self, e, fn, reads=(), writes=()):
        if self.mute:
            return None
        reads = [x.tile if isinstance(x, V) else x for x in reads]
        writes = [x.tile if isinstance(x, V) else x for x in writes]
        self._deps(e, reads, writes)
        if self.ecnt[e] >= self.EPOCH:
            self.esem[e] = self._new_sem()
            self.ecnt[e] = 0
        inst = fn(self.eng[e])
        inst.then_inc(self.esem[e], 1)
        self.ecnt[e] += 1
        self.ninst += 1
        tok = (self.esem[e], self.ecnt[e])
        self._mark(e, tok, reads, writes)
        return tok

    def dma(self, q, out, in_, is_output=False, **kw):
        if self.mute:
            return None
        reads = [in_.tile]
        writes = [out.tile]
        self._deps(q, reads, writes)
        half = self.NDSEM // 2
        if q == "pool":
            i = half + self.dpi % half
            self.dpi += 1
        else:
            i = self.di % half
            self.di += 1
        if self.dval[i] > 0:
            self._wait(q, (self.dsem[i], self.dval[i]))
        inst = self.eng[q].dma_start(out=out.a, in_=in_.a, **kw)
        inst.then_inc(self.dsem[i], 16)
        self.dval[i] += 16
        self.ninst += 1
        tok = (self.dsem[i], self.dval[i])
        self._mark("d%d" % i, tok, reads, writes)
        if is_output:
            self.out_tokens.append(tok)
        return tok

    def barrier(self):
        if self.mute:
            return
        toks = [(self.esem[e], self.ecnt[e]) for e in self.eng if self.ecnt[e] > 0]
        toks += [(self.dsem[i], self.dval[i]) for i in range(self.NDSEM) if self.dval[i] > 0]
        for e in self.eng:
            for tok in toks:
                self._wait(e, tok)

    def finish(self):
        if self.mute:
            return
        for tok in self.out_tokens:
            self._wait("sp", tok)
        for i in range(self.NDSEM):
            if self.dval[i] > 0:
                self._wait("sp", (self.dsem[i], self.dval[i]))

    def mm(self, out, lhsT, rhs, start=True, stop=True, **kw):
        return self.op("pe", lambda e: e.matmul(_ap(out), _ap(lhsT), _ap(rhs), start=start, stop=stop, **kw),
                       reads=[lhsT, rhs], writes=[out])

    def tr(self, out, in_, ident):
        return self.op("pe", lambda e: e.transpose(_ap(out), _ap(in_), _ap(ident)),
                       reads=[in_, ident], writes=[out])

    def act(self, out, in_, func, bias=None, scale=None, accum_out=None, eng="act"):
        kw = {}
        rd = [in_]
        wr = [out]
        if bias is not None:
            kw["bias"] = _ap(bias)
            if isinstance(bias, V):
                rd.append(bias)
        if scale is not None:
            kw["scale"] = _ap(scale)
            if isinstance(scale, V):
                rd.append(scale)
        if accum_out is not None:
            kw["accum_out"] = _ap(accum_out)
            wr.append(accum_out)
        return self.op(eng, lambda e: e.activation(_ap(out), _ap(in_), func, **kw), reads=rd, writes=wr)

    def tt(self, out, in0, in1, op, eng="dve"):
        return self.op(eng, lambda e: e.tensor_tensor(_ap(out), _ap(in0), _ap(in1), op),
                       reads=[in0, in1], writes=[out])

    def ts(self, out, in0, s1, op0, s2=None, op1=None, eng="dve", accum_out=None):
        rd = [in0]
        wr = [out]
        if isinstance(s1, V):
            rd.append(s1)
        if isinstance(s2, V):
            rd.append(s2)
        kw = {}
        if op1 is not None:
            kw["op1"] = op1
        if accum_out is not None:
            kw["accum_out"] = _ap(accum_out)
            wr.append(accum_out)
        return self.op(eng, lambda e: e.tensor_scalar(_ap(out), _ap(in0), _ap(s1), _ap(s2), op0, **kw),
                       reads=rd, writes=wr)

    def stt(self, out, in0, scalar, in1, op0, op1, eng="dve"):
        rd = [in0, in1]
        if isinstance(scalar, V):
            rd.append(scalar)
        return self.op(eng, lambda e: e.scalar_tensor_tensor(_ap(out), _ap(in0), _ap(scalar), _ap(in1), op0, op1),
                       reads=rd, writes=[out])

    def copy(self, out, in_, eng="dve"):
        if eng == "act":
            return self.act(out, in_, AF.Copy)
        return self.op(eng, lambda e: e.tensor_copy(_ap(out), _ap(in_)), reads=[in_], writes=[out])

    def memset(self, out, val, eng="pool"):
        return self.op(eng, lambda e: e.memset(_ap(out), val), writes=[out])

    def recip(self, out, in_, eng="dve"):
        return self.op(eng, lambda e: e.reciprocal(_ap(out), _ap(in_)), reads=[in_], writes=[out])

    def reduce(self, out, in_, op=ALU.add, axis=AX.X, eng="dve"):
        return self.op(eng, lambda e: e.tensor_reduce(_ap(out), _ap(in_), axis, op), reads=[in_], writes=[out])


from concourse.bass_utils import run_bass_kernel_spmd

D = 1024; L = 2; DFF = 2816; EPS = 1e-6
TS = 2048; TC = 512; PAST = 512
O_QA, O_KA, O_VA, O_CQ, O_CKV, O_KPE, O_QKV, O_Z, O_A, O_B, O_G = 0, 512, 640, 768, 1152, 1408, 1440, 2976, 3488, 3504, 3520
PERM64 = np.array(list(range(16, 32)) + list(range(0, 16)) + list(range(48, 64)) + list(range(32, 48)))
PERM32 = np.array(list(range(8, 16)) + list(range(0, 8)) + list(range(24, 32)) + list(range(16, 24)))
GRP = {}
_off = 0
for _n, _w in [("AQ", 512), ("AQS", 512), ("AK", 256), ("AKS", 256), ("AV", 128), ("AKT", 128), ("BCQ", 384),
               ("BCKV", 256), ("BKPE1", 96), ("BKPE2", 96), ("BKPET", 32), ("CQKV", 1536), ("CZ", 512), ("CAB", 32),
               ("G", 3072)]:
    GRP[_n] = (_off, _w)
    _off += _w
NCOL = _off


def _prep_w_in(w_in):
    cols = []
    qa = np.arange(O_QA, O_QA + 512)
    cols.append(qa)
    cols.append((qa.reshape(8, 64)[:, PERM64]).reshape(-1))
    ka = np.arange(O_KA, O_KA + 128).reshape(2, 64)
    kdup = np.concatenate([ka[0], ka[0], ka[1], ka[1]])
    cols.append(kdup)
    kas = ka[:, PERM64]
    cols.append(np.concatenate([kas[0], kas[0], kas[1], kas[1]]))
    cols.append(np.arange(O_VA, O_VA + 128))
    cols.append(np.arange(O_KA, O_KA + 128))
    cols.append(np.arange(O_CQ, O_CQ + 384))
    cols.append(np.arange(O_CKV, O_CKV + 256))
    kpe = np.arange(O_KPE, O_KPE + 32)
    Z = -1 * np.ones(64, dtype=np.int64)
    cols.append(np.concatenate([Z, kpe]))
    cols.append(np.concatenate([Z, kpe[PERM32]]))
    cols.append(kpe)
    cols.append(np.arange(O_QKV, O_QKV + 1536))
    cols.append(np.arange(O_Z, O_Z + 512))
    cols.append(np.arange(O_A, O_A + 32))
    cols.append(np.arange(O_G, O_G + 3072))
    idx = np.concatenate(cols)
    assert idx.shape[0] == NCOL
    wz = np.concatenate([w_in, np.zeros(w_in.shape[:2] + (1,), np.float32)], axis=2)
    return np.ascontiguousarray(wz[:, :, idx])


def _tables(T, rope):
    tA = np.zeros((2, 128, T), np.float32)
    tB = np.zeros((2, 128, T), np.float32)
    if not rope:
        tA[0] = 1.0
        tB[0, 64:96] = 1.0
        return tA, tB
    t = np.arange(T)
    row = (t // 64).astype(np.float32)
    col = (t % 64).astype(np.float32)
    for (tab, nf, base) in ((tA, 16, 0), (tB, 8, 64)):
        inv = (np.float32(10000.0) ** (-np.arange(nf, dtype=np.float32) / np.float32(nf))).astype(np.float32)
        ar = (row[:, None] * inv[None, :]).astype(np.float32)
        ac = (col[:, None] * inv[None, :]).astype(np.float32)
        cr, sr, cc, sc = np.cos(ar).T, np.sin(ar).T, np.cos(ac).T, np.sin(ac).T
        cosp = np.concatenate([cr, cr, cc, cc], 0).astype(np.float32)
        sinp = np.concatenate([-sr, sr, -sc, sc], 0).astype(np.float32)
        n = 4 * nf
        tab[0, base:base + n] = cosp
        tab[1, base:base + n] = sinp
        if base == 0:
            tab[0, 64:128] = cosp
            tab[1, 64:128] = sinp
    return tA, tB


CONST = {}
_coff = 0
for _n, _w in [("ident", 128), ("blk2", 128), ("ones", 128), ("CMf", 1024), ("CMb", 1024), ("NEGMf", 512),
               ("NEGMb", 512), ("I8", 512), ("Uf", 64), ("Ub", 64), ("nUf", 64), ("nUb", 64), ("noff", 64), ("MS", 384)]:
    CONST[_n] = (_coff, _w)
    _coff += _w
NCONST = _coff


def _consts():
    c = np.zeros((128, NCONST), np.float32)

    def put(n, a):
        o, w = CONST[n]
        a = np.asarray(a, np.float32).reshape(a.shape[0], -1)
        assert a.shape[1] == w
        c[:a.shape[0], o:o + w] = a
    put("ident", np.eye(128))
    b = np.zeros((128, 128)); b[:64, :64] = 1; b[64:, 64:] = 1
    put("blk2", b)
    put("ones", np.ones((128, 128)))
    k = np.arange(64)[:, None]; f = np.arange(64)[None, :]
    Uf = (k <= f).astype(np.float32); Ub = (k >= f).astype(np.float32)
    for nm, U in (("CMf", Uf), ("CMb", Ub)):
        cm = np.zeros((64, 2, 8, 64), np.float32)
        cm[:, 0] = U[:, None, :]
        cm[:, 1] = 1.0
        put(nm, cm)
    put("NEGMf", np.broadcast_to(np.where(k <= f, 0.0, -30000.0)[:, None, :], (64, 8, 64)).copy())
    put("NEGMb", np.broadcast_to(np.where(k >= f, 0.0, -30000.0)[:, None, :], (64, 8, 64)).copy())
    put("I8", np.broadcast_to(np.eye(64)[:, None, :], (64, 8, 64)).copy())
    put("Uf", Uf); put("Ub", Ub); put("nUf", -Uf); put("nUb", -Ub)
    put("noff", -(1.0 - np.eye(64)))
    ms = np.zeros((64, 6, 64), np.float32)
    pi = np.arange(64)[:, None]; fi = np.arange(64)[None, :]
    for lv in range(6):
        ms[:, lv, :] = ((pi >> (lv + 1)) == (fi >> (lv + 1))) & (((pi >> lv) & 1) != ((fi >> lv) & 1))
    put("MS", ms)
    return c


IN_SPECS = [
    ("xs", (TS, D)), ("xc", (TC, D)), ("cka", (L, PAST, 128)), ("cva", (L, PAST, 128)), ("cckv", (L, PAST, 256)),
    ("ckpe", (L, PAST, 32)), ("sf", (L, 8, 64, 64)), ("sb", (L, 8, 64, 64)), ("cv2", (2, D)),
    ("w_mod", (L, D, 6 * D)), ("b_mod", (L, 6 * D)), ("norm1", (L, D)), ("norm2", (L, D)), ("w_in_r", (L, D, NCOL)),
    ("aq", (L, 2, 64)), ("akn", (L, 2, 64)), ("b_qnorm", (L, 384)), ("b_kvnorm", (L, 256)),
    ("w_uq", (L, 384, 768)), ("w_uqs", (L, 384, 768)), ("w_ukv_k", (L, 256, 512)), ("w_ukv_v", (L, 256, 512)),
    ("c_conv", (L, 3, 1536)), ("c_alog", (L, 16)), ("c_dt", (L, 16)), ("c_onorm", (L, 64)),
    ("w_pa", (L, 512, D)), ("w_pb", (L, 512, D)), ("w_pc", (L, 512, D)), ("w_out", (L, D, D)),
    ("w_gate", (L, D, DFF)), ("w_up", (L, D, DFF)), ("w_down", (L, DFF, D)), ("final_norm", (D,)),
    ("tabA_S", (2, 128, TS)), ("tabB_S", (2, 128, TS)), ("tabA_C", (2, 128, TC)), ("tabB_C", (2, 128, TC)),
    ("consts", (128, NCONST)),
]
OUT_SPECS = [
    ("ys", (TS, D)), ("yc", (TC, D)), ("nka", (2, L, 256, 128)), ("nva", (2, L, 256, 128)),
    ("nckv", (2, L, 256, 256)), ("nkpe", (2, L, 256, 32)), ("nsf", (2, L, 8, 64, 64)), ("nsb", (2, L, 8, 64, 64)),
]


class PassCfg:
    def __init__(self, name):
        self.name = name
        if name == "S":
            self.T = TS; self.seqs = [(0, TS)]; self.ncache = PAST; self.row = 0
            self.blocks = [(0, i * 512, 512) for i in range(4)]
        else:
            self.T = TC; self.seqs = [(0, 256), (256, 256)]; self.ncache = 0; self.row = 32
            self.blocks = [(0, 0, 256), (1, 256, 256)]
        self.nch = self.T // 64


class _Stop(Exception):
    pass


class Builder:
    def stage(self, k):
        if self.cfg.get("stopat") == k and not self.kb.mute:
            self.kb.barrier()
            self.kb.finish()
            self.kb.mute = True

    def __init__(self, cfg=None):
        self.cfg = cfg or {}
        nc = self.nc = bass.Bass("TRN2", target_bir_lowering=False)
        self.I = {n: nc.dram_tensor(n, list(s), F32, kind="ExternalInput") for n, s in IN_SPECS}
        self.O = {n: nc.dram_tensor(n, list(s), F32, kind="ExternalOutput") for n, s in OUT_SPECS}
        self.modD = nc.dram_tensor("modD", [L, 33, 6 * D], F32, kind="Internal")
        self.xbuf = {"S": nc.dram_tensor("xbufS", [TS, D], F32, kind="Internal"),
                     "C": nc.dram_tensor("xbufC", [TC, D], F32, kind="Internal")}
        self.oD = {(pn, d): nc.dram_tensor("oD%s%d" % (pn, d), [T // 64, 64, 512], F32, kind="Internal")
                   for pn, T in (("S", TS), ("C", TC)) for d in (0, 1)}
        self.qkvD = {pn: nc.dram_tensor("qkvD%s" % pn, [128, 12, T], BF16, kind="Internal")
                     for pn, T in (("S", TS), ("C", TC))}
        self.wi = 0
        self.oi = 0
        self.pool_i = {}
        self.ring_i = {}

    def d(self, ap):
        return V(Tile("d", None), ap)

    def phase(self):
        b = self

        class _P:
            def __enter__(s):
                s.old = b.kb.st
                s.es = ExitStack()
                s.es.__enter__()
                b.kb.st = s.es
                return s

            def __exit__(s, *a):
                b.kb.barrier()
                b.kb.st = s.old
                return s.es.__exit__(*a)
        return _P()

    def pbank(self, pool):
        banks = {"g": [0, 1, 2, 3], "s": [0, 1, 2, 3], "t": [4, 5], "o": [6, 7],
                 "g0": [0, 1, 2], "g1": [3, 4, 5], "t0": [6], "t1": [7]}[pool]
        i = self.pool_i.get(pool, 0)
        self.pool_i[pool] = i + 1
        return self.PS[banks[i % len(banks)]]

    def ring(self, name, tiles):
        i = self.ring_i.get(name, 0)
        self.ring_i[name] = i + 1
        return tiles[i % len(tiles)]

    def bfv(self, ps):
        return V(ps, ps.t[:].bitcast(BF16))

    def alloc_wring(self):
        self.wring = [self.kb.sb("wr%d" % i, [128, 8, 512], BF16) for i in range(4)]
        self.wi = 0

    def wload(self, src_ap, K, n):
        t = self.wring[self.wi % len(self.wring)]
        self.wi += 1
        kc = K // 128
        self.kb.dma("pool", t[:, 0:kc, 0:n], self.d(src_ap.rearrange("(kc p) n -> p kc n", p=128)))
        return t

    def wstream(self, specs, ahead=2):
        tiles = []
        for i in range(min(ahead, len(specs))):
            tiles.append(self.wload(*specs[i]))
        for i in range(len(specs)):
            if i + ahead < len(specs):
                tiles.append(self.wload(*specs[i + ahead]))
            yield tiles[i]

    def win(self, l, grp, c0=0, n=None):
        o, w = GRP[grp]
        n = w - c0 if n is None else n
        return (self.I["w_in_r"][l, :, o + c0:o + c0 + n], D, n)

    def setup(self):
        kb = self.kb
        self.PS = [kb.ps("ps%d" % i, [128, 512], F32) for i in range(8)]
        cst = self.cst = kb.sb("cst", [128, 384], F32)
        kb.dma("sp", cst.ap(), self.d(self.I["consts"][:, 0:384]))
        self.ident_bf = kb.sb("ident_bf", [128, 128], BF16)
        self.blk2_bf = kb.sb("blk2_bf", [128, 128], BF16)
        self.ones_bf = kb.sb("ones_bf", [128, 128], BF16)
        for t, n in ((self.ident_bf, "ident"), (self.blk2_bf, "blk2"), (self.ones_bf, "ones")):
            o, w = CONST[n]
            kb.copy(t.ap(), cst[:, o:o + w])
        self.xt_ring = [kb.sb("xt%d" % i, [128, D], F32) for i in range(1)]
        self.xn_ring = [kb.sb("xn%d" % i, [128, D], BF16) for i in range(1)]
        self.junk_bf = kb.sb("junk", [128, D], BF16)
        self.small = [kb.sb("sm%d" % i, [128, 8], F32) for i in range(6)]

    def cv(self, n, rows=128):
        o, w = CONST[n]
        return self.cst[0:rows, o:o + w]

    def mods(self):
        kb = self.kb
        with self.phase():
            craw = kb.sb("craw", [128, 2, 8], F32)
            for r in range(2):
                kb.dma("sp", craw[:, r, :], self.d(self.I["cv2"][r, :].rearrange("(kc p) -> p kc", p=128)),
                       allow_slow_non_contiguous=True)
            sc33 = kb.sb("sc33", [128, 8, 33], F32)
            kb.memset(sc33.ap(), 0.0, eng="dve")
            kb.act(sc33[:, :, 0:1], craw[:, 0, :].re("p (k o) -> p k o", o=1), AF.Silu)
            kb.act(sc33[:, :, 32:33], craw[:, 1, :].re("p (k o) -> p k o", o=1), AF.Silu)
            wm = [kb.sb("wm%d" % i, [128, 8, 512], F32) for i in range(2)]
            mrow = [kb.sb("mrow%d" % i, [33, 512], F32) for i in range(2)]
            for l in range(L):
                for j in range(12):
                    w = wm[j % 2]
                    kb.dma("sp", w.ap(), self.d(self.I["w_mod"][l, :, j * 512:(j + 1) * 512].rearrange("(kc p) n -> p kc n", p=128)))
                    ps = self.pbank("g")
                    for kc in range(8):
                        kb.mm(ps[0:33, :], sc33[:, kc, :], w[:, kc, :], start=(kc == 0), stop=(kc == 7))
                    m = mrow[j % 2]
                    kb.copy(m.ap(), ps[0:33, :], eng="act" if j % 2 else "dve")
                    kb.dma("sp", V(self.modT, self.modD[l, :, j * 512:(j + 1) * 512]), m.ap())

    def layer_consts(self, p, l):
        kb = self.kb
        r = p.row
        mc = kb.sb("mc", [128, 48], F32)
        bc = kb.sb("bc", [128, 48], F32)
        kb.dma("sp", mc.ap(), V(self.modT, self.modD[l, r, :].rearrange("(m q) -> q m", q=128)), allow_slow_non_contiguous=True)
        kb.dma("sp", bc.ap(), self.d(self.I["b_mod"][l, :].rearrange("(m q) -> q m", q=128)), allow_slow_non_contiguous=True)
        kb.tt(mc.ap(), mc.ap(), bc.ap(), ALU.add)
        ncol = kb.sb("ncol", [128, 16], F32)
        kb.dma("sp", ncol[:, 0:8], self.d(self.I["norm1"][l, :].rearrange("(c q) -> q c", q=128)), allow_slow_non_contiguous=True)
        kb.dma("sp", ncol[:, 8:16], self.d(self.I["norm2"][l, :].rearrange("(c q) -> q c", q=128)), allow_slow_non_contiguous=True)
        AB = kb.sb("AB", [128, 32], F32)
        kb.stt(AB[:, 0:8], mc[:, 8:16], 1.0, ncol[:, 0:8], ALU.add, ALU.mult)
        kb.copy(AB[:, 8:16], mc[:, 0:8])
        kb.stt(AB[:, 16:24], mc[:, 32:40], 1.0, ncol[:, 8:16], ALU.add, ALU.mult)
        kb.copy(AB[:, 24:32], mc[:, 24:32])
        return AB

    def make_gb(self, p, l):
        kb = self.kb
        r = p.row
        gb = kb.sb("gb", [128, 2, D], F32)
        gtmp = self.xt_ring[0]
        for i, c0 in enumerate((2 * D, 5 * D)):
            kb.dma("sp", gb[:, i, :], V(self.modT, self.modD[l, r:r + 1, c0:c0 + D].to_broadcast([128, D])))
            kb.dma("sp", gtmp.ap(), self.d(self.I["b_mod"][l:l + 1, c0:c0 + D].to_broadcast([128, D])))
            kb.tt(gb[:, i, :], gb[:, i, :], gtmp.ap(), ALU.add)
        return gb

    def norm_one(self, xt, Acol, Bcol, hT, t0):
        kb = self.kb
        sm = self.ring("sm", self.small)
        kb.act(self.junk_bf.ap(), xt.ap(), AF.Square, accum_out=sm[:, 0:1])
        kb.act(sm[:, 1:2], sm[:, 0:1], AF.Sqrt, scale=1.0 / D, bias=EPS)
        kb.recip(sm[:, 2:3], sm[:, 1:2])
        xn = self.ring("xn", self.xn_ring)
        kb.act(xn.ap(), xt.ap(), AF.Identity, scale=sm[:, 2:3])
        pb = self.bfv(self.pbank("t"))
        for c in range(8):
            kb.tr(pb[:, c * 128:(c + 1) * 128], xn[:, c * 128:(c + 1) * 128], self.ident_bf.ap())
        for c in range(8):
            if c % 2 == 0:
                kb.ts(hT[:, c, t0:t0 + 128], pb[:, c * 128:(c + 1) * 128], Acol[:, c:c + 1], ALU.mult, Bcol[:, c:c + 1], ALU.add)
            else:
                kb.act(hT[:, c, t0:t0 + 128], pb[:, c * 128:(c + 1) * 128], AF.Identity, scale=Acol[:, c:c + 1], bias=Bcol[:, c:c + 1])
        return sm

    def xsrc(self, p, l, i):
        if l == 0:
            return self.d(self.I["xs" if p.name == "S" else "xc"][i * 128:(i + 1) * 128, :])
        return V(self.xT[p.name][i], self.xbuf[p.name][i * 128:(i + 1) * 128, :])

    def rms_rstd(self, px, n, lhs_ones, inv_n, first=True, last=True, pss=None):
        kb = self.kb
        sq = self.ring("sq", self.sq_ring)
        kb.act(sq[:, 0:n], px, AF.Square)
        if pss is None:
            pss = self.pbank("g")
        kb.mm(pss[:, 0:n], lhs_ones, sq[:, 0:n], start=first, stop=last)
        return pss

    def rstd_from(self, pss, n, inv_n):
        kb = self.kb
        rs = self.ring("rs", self.rs_ring)
        kb.act(rs[:, 0:n], pss[:, 0:n], AF.Sqrt, scale=inv_n, bias=EPS)
        kb.recip(rs[:, 0:n], rs[:, 0:n])
        return rs

    def rope_norm(self, px, pxs, n, lhs_ones, inv_n, gcols, cos, sin, out, rows=slice(0, 128)):
        kb = self.kb
        pss = self.rms_rstd(px[:, 0:n], n, lhs_ones, inv_n)
        rs = self.rstd_from(pss, n, inv_n)
        t1 = self.ring("t1", self.t1_ring)
        t2 = self.ring("t2", self.t2_ring)
        kb.tt(t1[:, 0:n], px[:, 0:n], rs[:, 0:n], ALU.mult)
        kb.stt(t1[:, 0:n], t1[:, 0:n], gcols[:, 0:1], cos, ALU.mult, ALU.mult)
        kb.tt(t2[:, 0:n], pxs[:, 0:n], rs[:, 0:n], ALU.mult)
        kb.stt(t2[:, 0:n], t2[:, 0:n], gcols[:, 1:2], sin, ALU.mult, ALU.mult)
        kb.tt(out, t1[:, 0:n], t2[:, 0:n], ALU.add, eng="pool")

    def pfm(self, ps, w, col0, M, actT, nkc, t0, n):
        for kc in range(nkc):
            self.kb.mm(ps, w[:, kc, col0:col0 + M], actT[:, kc, t0:t0 + n], start=(kc == 0), stop=(kc == nkc - 1))

    def attend(self, qv, kv, vv, nkt, Lq, scale, yT, tok0, hq):
        kb = self.kb
        r0 = (hq % 2) * 64
        c = hq // 2
        for q0 in range(0, Lq, 512):
            n = min(512, Lq - q0)
            nj = n // 128
            po = self.pbank("o")
            pTs = {}
            for kt in range(nkt + 1):
                if kt < nkt:
                    pst = self.pbank("s")
                    kb.mm(pst[:, 0:n], kv(kt), qv(q0, n))
                    pT = self.ring("pT", self.pT_ring)
                    kb.act(pT[:, 0:n], pst[:, 0:n], AF.Exp, scale=scale)
                    pTs[kt] = pT
                if kt >= 1:
                    k2 = kt - 1
                    pT = pTs.pop(k2)
                    for j in range(nj):
                        kb.mm(po[:, j * 65:(j + 1) * 65], pT[:, j * 128:(j + 1) * 128], vv(k2),
                              start=(k2 == 0 and j == 0), stop=(k2 == nkt - 1), skip_group_check=True)
            pb = self.bfv(self.pbank("t"))
            for j in range(nj):
                rc = self.ring("sm", self.small)
                kb.recip(rc[:, 0:1], po[:, j * 65 + 64:j * 65 + 65])
                yn = self.ring("yn", self.yn_ring)
                kb.ts(yn.ap(), po[:, j * 65:j * 65 + 64], rc[:, 0:1], ALU.mult)
                kb.tr(pb[0:64, j * 128:(j + 1) * 128], yn.ap(), self.ident_bf.ap())
            self.cp_i = getattr(self, "cp_i", 0) + 1
            kb.copy(yT[r0:r0 + 64, c, tok0 + q0:tok0 + q0 + n], pb[0:64, 0:n], eng="act" if self.cp_i % 2 else "dve")

    def alloc_attn_common(self, p):
        kb = self.kb
        self.sq_ring = [kb.sb("sq%d" % i, [128, 512], BF16) for i in range(2)]
        self.rs_ring = [kb.sb("rs%d" % i, [128, 512], F32) for i in range(1)]
        self.t1_ring = [kb.sb("t1%d" % i, [128, 512], F32) for i in range(1)]
        self.t2_ring = [kb.sb("t2%d" % i, [128, 512], F32) for i in range(1)]
        self.pT_ring = [kb.sb("pT%d" % i, [128, 512], BF16) for i in range(4)]
        self.cos_ring = [kb.sb("cos%d" % i, [128, 512], F32) for i in range(1)]
        self.sin_ring = [kb.sb("sin%d" % i, [128, 512], F32) for i in range(1)]
        self.yn_ring = [kb.sb("yn%d" % i, [128, 64], BF16) for i in range(4)]
        self.yT = kb.sb("yT", [128, 4, p.T], BF16)
        self.stg = [kb.sb("stg%d" % i, [128, 512], F32) for i in range(1)]

    def load_tabs(self, p, which, t0, n):
        kb = self.kb
        tab = self.I["tab%s_%s" % (which, p.name)]
        cos = self.ring("cos", self.cos_ring)
        sin = self.ring("sin", self.sin_ring)
        kb.dma("sp", cos[:, 0:n], self.d(tab[0, :, t0:t0 + n]))
        kb.dma("sp", sin[:, 0:n], self.d(tab[1, :, t0:t0 + n]))
        return cos, sin

    def mixerA(self, p, l, hT):
        kb = self.kb
        I = self.I
        T = p.T
        nseq = len(p.seqs)
        Ls = p.seqs[0][1]
        NK = p.ncache + Ls
        nkt = NK // 128
        self.alloc_attn_common(p)
        qT = kb.sb("qT", [128, 4, T], BF16)
        kT = [kb.sb("kT%d" % g, [128, nseq, NK], BF16) for g in range(2)]
        vA = kb.sb("vA", [128, nseq * nkt, 2, 65], BF16)
        kb.memset(vA[:, :, :, 64:65], 1.0, eng="dve")
        gq = kb.sb("gq", [128, 2], F32)
        gk = kb.sb("gk", [128, 2], F32)
        gkrow = kb.sb("gkrow", [128, 64], F32)
        for hf in range(2):
            kb.dma("sp", gq[hf * 64:(hf + 1) * 64, :], self.d(I["aq"][l].rearrange("r d -> d r")), allow_slow_non_contiguous=True)
            kb.dma("sp", gk[hf * 64:(hf + 1) * 64, :], self.d(I["akn"][l].rearrange("r d -> d r")), allow_slow_non_contiguous=True)
        kb.dma("sp", gkrow.ap(), self.d(I["akn"][l, 0:1, :].to_broadcast([128, 64])))
        if p.ncache:
            cbits = self.cfg.get("cbits", 7)
            for kt in range(p.ncache // 128):
                ck = self.ring("stg", self.stg)
                cd = self.ring("xn", self.xn_ring)
                if cbits & 1:
                    kb.dma("sp", ck[:, 0:128], self.d(I["cka"][l, kt * 128:(kt + 1) * 128, :]))
                    kb.copy(cd[:, 0:256].re("p (g r d) -> p g r d", g=2, r=2), ck[:, 0:128].re("p (g o d) -> p g o d", g=2, o=1).bc([128, 2, 2, 64]))
                if cbits & 2:
                    pb = self.bfv(self.pbank("t"))
                    for g in range(2):
                        kb.tr(pb[:, g * 128:(g + 1) * 128], cd[:, g * 128:(g + 1) * 128], self.ident_bf.ap())
                    for g in range(2):
                        kb.ts(kT[g][:, 0, kt * 128:(kt + 1) * 128], pb[:, g * 128:(g + 1) * 128], 1.0, ALU.mult)
                if cbits & 4:
                    kb.dma("sp", ck[:, 128:256], self.d(I["cva"][l, kt * 128:(kt + 1) * 128, :]))
                    kb.copy(vA[:, kt, :, 0:64], ck[:, 128:256].re("p (g d) -> p g d", g=2), eng="act")
        self.stage(3)
        wq, wqs, wk, wv = [self.wload(*s) for s in (self.win(l, "AQ"), self.win(l, "AQS"), self.win(l, "AK", 0, 512), self.win(l, "AV", 0, 256))]
        for (s, t0, n) in p.blocks:
            s0 = p.seqs[s][0]
            cos, sin = self.load_tabs(p, "A", t0, n)
            for c in range(4):
                px = self.pbank("g"); pxs = self.pbank("g")
                self.pfm(px[:, 0:n], wq, c * 128, 128, hT, 8, t0, n)
                self.pfm(pxs[:, 0:n], wqs, c * 128, 128, hT, 8, t0, n)
                self.rope_norm(px, pxs, n, self.blk2_bf.ap(), 1.0 / 64, gq, cos[:, 0:n], sin[:, 0:n], qT[:, c, t0:t0 + n])
            for g in range(2):
                px = self.pbank("g"); pxs = self.pbank("g")
                self.pfm(px[:, 0:n], wk, g * 128, 128, hT, 8, t0, n)
                self.pfm(pxs[:, 0:n], wk, 256 + g * 128, 128, hT, 8, t0, n)
                k0 = p.ncache + t0 - s0
                self.rope_norm(px, pxs, n, self.ones_bf.ap(), 1.0 / 128, gk, cos[:, 0:n], sin[:, 0:n], kT[g][:, s, k0:k0 + n])
            for i in range(n // 128):
                tt0 = t0 + i * 128
                pv = self.pbank("g")
                for kc in range(8):
                    kb.mm(pv[:, 0:256], hT[:, kc, tt0:tt0 + 128], wv[:, kc, 0:256], start=(kc == 0), stop=(kc == 7))
                kt = s * nkt + (p.ncache + tt0 - s0) // 128
                kb.copy(vA[:, kt, :, 0:64], pv[:, 0:128].re("p (g d) -> p g d", g=2), eng="act")
                if p.name == "C":
                    st = self.ring("stg", self.stg)
                    kb.copy(st[:, 0:128], pv[:, 0:128])
                    kb.dma("sp", self.d(self.O["nva"][s, l, tt0 - s0:tt0 - s0 + 128, :]), st[:, 0:128], is_output=True)
                    sm = self.ring("sm", self.small)
                    sqf = self.ring("t1", self.t1_ring)
                    kb.act(sqf[:, 0:128], pv[:, 128:256], AF.Square)
                    kb.reduce(sm[:, 0:2], sqf[:, 0:128].re("p (g d) -> p g d", g=2))
                    kb.act(sm[:, 2:4], sm[:, 0:2], AF.Sqrt, scale=1.0 / 64, bias=EPS)
                    kb.recip(sm[:, 2:4], sm[:, 2:4])
                    kb.tt(st[:, 128:256].re("p (g d) -> p g d", g=2), pv[:, 128:256].re("p (g d) -> p g d", g=2),
                          sm[:, 2:4].re("p (g o) -> p g o", o=1).bc([128, 2, 64]), ALU.mult)
                    kb.tt(st[:, 128:256].re("p (g d) -> p g d", g=2), st[:, 128:256].re("p (g d) -> p g d", g=2),
                          gkrow.ap().re("p (o d) -> p o d", o=1).bc([128, 2, 64]), ALU.mult)
                    kb.dma("sp", self.d(self.O["nka"][s, l, tt0 - s0:tt0 - s0 + 128, :]), st[:, 128:256], is_output=True)
        self.stage(4)
        for s, (s0, Lq) in enumerate(p.seqs):
            for hq in range(self.cfg.get("nheads", 8)):
                g = hq // 4
                r0 = (hq % 2) * 64
                c = hq // 2
                self.attend(lambda q0, n: qT[r0:r0 + 64, c, s0 + q0:s0 + q0 + n],
                            lambda kt: kT[g][r0:r0 + 64, s, kt * 128:(kt + 1) * 128],
                            lambda kt: vA[:, s * nkt + kt, g, :],
                            nkt, Lq, 0.125, self.yT, s0, hq)
        return self.yT

    def mixerB(self, p, l, hT):
        kb = self.kb
        I = self.I
        T = p.T
        nseq = len(p.seqs)
        Ls = p.seqs[0][1]
        NK = p.ncache + Ls
        nkt = NK // 128
        self.alloc_attn_common(p)
        cqn = kb.sb("cqn", [128, 3, T], BF16)
        ckvn = kb.sb("ckvn", [128, 2, nseq, NK], BF16)
        kB = kb.sb("kB", [128, nseq, NK], BF16)
        kpeT = kB
        vB = kb.sb("vB", [128, nseq * nkt, 8, 65], BF16)
        kb.memset(vB[:, :, :, 64:65], 1.0, eng="dve")
        gqc = kb.sb("gqc", [128, 3], F32)
        gkc = kb.sb("gkc", [128, 2], F32)
        gkrow = kb.sb("gkvrow", [128, 256], F32)
        kb.dma("sp", gqc.ap(), self.d(I["b_qnorm"][l, :].rearrange("(c q) -> q c", q=128)), allow_slow_non_contiguous=True)
        kb.dma("sp", gkc.ap(), self.d(I["b_kvnorm"][l, :].rearrange("(c q) -> q c", q=128)), allow_slow_non_contiguous=True)
        kb.dma("sp", gkrow.ap(), self.d(I["b_kvnorm"][l:l + 1, :].to_broadcast([128, 256])))
        if p.ncache:
            for kt in range(p.ncache // 128):
                ck = self.ring("stg", self.stg)
                kb.dma("sp", ck[:, 0:256], self.d(I["cckv"][l, kt * 128:(kt + 1) * 128, :]))
                kb.dma("sp", ck[:, 256:288], self.d(I["ckpe"][l, kt * 128:(kt + 1) * 128, :]))
                cd = self.ring("xn", self.xn_ring)
                kb.memset(cd[:, 256:384], 0.0, eng="dve")
                kb.copy(cd[:, 0:256], ck[:, 0:256])
                kb.copy(cd[:, 320:352], ck[:, 256:288])
                pb = self.bfv(self.pbank("t"))
                for c in range(3):
                    kb.tr(pb[:, c * 128:(c + 1) * 128], cd[:, c * 128:(c + 1) * 128], self.ident_bf.ap())
                kb.ts(ckvn[:, :, 0, kt * 128:(kt + 1) * 128], pb[:, 0:256].re("p (c t) -> p c t", c=2), 1.0, ALU.mult)
                kb.ts(kpeT[64:96, 0, kt * 128:(kt + 1) * 128], pb[64:96, 256:384], 1.0, ALU.mult)
        wcq = self.wload(*self.win(l, "BCQ"))
        wkv = self.wload(*self.win(l, "BCKV", 0, 256 + 96 + 96 + 32))
        for (s, t0, n) in p.blocks:
            s0 = p.seqs[s][0]
            k0 = p.ncache + t0 - s0
            cos, sin = self.load_tabs(p, "B", t0, n)
            pxs = [self.pbank("g") for _ in range(3)]
            pss = self.pbank("g")
            for c in range(3):
                self.pfm(pxs[c][:, 0:n], wcq, c * 128, 128, hT, 8, t0, n)
                self.rms_rstd(pxs[c][:, 0:n], n, self.ones_bf.ap(), 0, first=(c == 0), last=(c == 2), pss=pss)
            rs = self.rstd_from(pss, n, 1.0 / 384)
            for c in range(3):
                kb.stt(cqn[:, c, t0:t0 + n], pxs[c][:, 0:n], gqc[:, c:c + 1], rs[:, 0:n], ALU.mult, ALU.mult)
            pxs = [self.pbank("g") for _ in range(2)]
            pss = self.pbank("g")
            for c in range(2):
                self.pfm(pxs[c][:, 0:n], wkv, c * 128, 128, hT, 8, t0, n)
                self.rms_rstd(pxs[c][:, 0:n], n, self.ones_bf.ap(), 0, first=(c == 0), last=(c == 1), pss=pss)
            rs = self.rstd_from(pss, n, 1.0 / 256)
            for c in range(2):
                kb.stt(ckvn[:, c, s, k0:k0 + n], pxs[c][:, 0:n], gkc[:, c:c + 1], rs[:, 0:n], ALU.mult, ALU.mult)
            p1 = self.pbank("g"); p2 = self.pbank("g")
            self.pfm(p1[0:96, 0:n], wkv, 256, 96, hT, 8, t0, n)
            self.pfm(p2[0:96, 0:n], wkv, 352, 96, hT, 8, t0, n)
            t1 = self.ring("t1", self.t1_ring); t2 = self.ring("t2", self.t2_ring)
            kb.tt(t1[64:96, 0:n], p1[64:96, 0:n], cos[64:96, 0:n], ALU.mult)
            kb.tt(t2[64:96, 0:n], p2[64:96, 0:n], sin[64:96, 0:n], ALU.mult)
            kb.tt(kpeT[64:96, s, k0:k0 + n], t1[64:96, 0:n], t2[64:96, 0:n], ALU.add, eng="pool")
            if p.name == "C":
                for i in range(n // 128):
                    tt0 = t0 + i * 128
                    pv = self.pbank("g")
                    for kc in range(8):
                        kb.mm(pv[:, 0:288], hT[:, kc, tt0:tt0 + 128], wkv[:, kc, 0:288], start=(kc == 0), stop=(kc == 7))
                    for kc in range(8):
                        kb.mm(pv[:, 288:320], hT[:, kc, tt0:tt0 + 128], wkv[:, kc, 448:480], start=False, stop=(kc == 7), skip_group_check=True)
                    st = self.ring("stg", self.stg)
                    sm = self.ring("sm", self.small)
                    kb.act(st[:, 0:256], pv[:, 0:256], AF.Square, accum_out=sm[:, 0:1])
                    kb.act(sm[:, 1:2], sm[:, 0:1], AF.Sqrt, scale=1.0 / 256, bias=EPS)
                    kb.recip(sm[:, 2:3], sm[:, 1:2])
                    kb.stt(st[:, 0:256], pv[:, 0:256], sm[:, 2:3], gkrow.ap(), ALU.mult, ALU.mult)
                    kb.dma("sp", self.d(self.O["nckv"][s, l, tt0 - s0:tt0 - s0 + 128, :]), st[:, 0:256], is_output=True)
                    kb.copy(st[:, 256:288], pv[:, 288:320], eng="act")
                    kb.dma("sp", self.d(self.O["nkpe"][s, l, tt0 - s0:tt0 - s0 + 128, :]), st[:, 256:288], is_output=True)
        wv = self.wload(I["w_ukv_v"][l, :, :], 256, 512)
        wk = self.wload(I["w_ukv_k"][l, :, :], 256, 512)
        wq = [self.wload(I["w_uq"][l, :, i * 384:(i + 1) * 384], 384, 384) for i in range(2)]
        for s in range(nseq):
            for kt in range(nkt):
                pv = self.pbank("g")
                for kc in range(2):
                    kb.mm(pv.ap(), ckvn[:, kc, s, kt * 128:(kt + 1) * 128], wv[:, kc, :], start=(kc == 0), stop=(kc == 1))
                kb.copy(vB[:, s * nkt + kt, :, 0:64], pv.ap().re("p (h d) -> p h d", h=8), eng="act" if kt % 2 else "dve")
        qBs = [kb.sb("qB%d" % i, [128, T], BF16) for i in range(1)]
        wqs_t = [None, None]
        for h in range(8):
            qB = qBs[0]
            if h == 0:
                wqs_t = [kb.sb("wqs", [128, 3, 768], BF16)]
                for i in range(2):
                    kb.dma("pool", wqs_t[0][:, :, i * 384:(i + 1) * 384], self.d(I["w_uqs"][l, :, i * 384:(i + 1) * 384].rearrange("(kc q) n -> q kc n", q=128)))
            for s in range(nseq):
                for b0 in range(0, NK, 512):
                    n = min(512, NK - b0)
                    pk = self.pbank("g")
                    for kc in range(2):
                        kb.mm(pk[0:64, 0:n], wk[:, kc, h * 64:(h + 1) * 64], ckvn[:, kc, s, b0:b0 + n], start=(kc == 0), stop=(kc == 1))
                    kb.copy(kB[0:64, s, b0:b0 + n], pk[0:64, 0:n], eng="act")
            for (s, t0, n) in p.blocks:
                cos, sin = self.load_tabs(p, "B", t0, n)
                p1 = self.pbank("g"); p2 = self.pbank("g")
                w1 = wq[h // 4]; c1 = (h % 4) * 96
                self.pfm(p1[0:96, 0:n], w1, c1, 96, cqn, 3, t0, n)
                self.pfm(p2[0:96, 0:n], wqs_t[0], h * 96, 96, cqn, 3, t0, n)
                kb.copy(qB[0:64, t0:t0 + n], p1[0:64, 0:n], eng="act")
                t1 = self.ring("t1", self.t1_ring); t2 = self.ring("t2", self.t2_ring)
                kb.tt(t1[64:96, 0:n], p1[64:96, 0:n], cos[64:96, 0:n], ALU.mult)
                kb.tt(t2[64:96, 0:n], p2[64:96, 0:n], sin[64:96, 0:n], ALU.mult)
                kb.tt(qB[64:96, t0:t0 + n], t1[64:96, 0:n], t2[64:96, 0:n], ALU.add, eng="pool")
            for s, (s0, Lq) in enumerate(p.seqs):
                self.attend(lambda q0, n: qB[0:96, s0 + q0:s0 + q0 + n],
                            lambda kt: kB[0:96, s, kt * 128:(kt + 1) * 128],
                            lambda kt: vB[:, s * nkt + kt, h, :],
                            nkt, Lq, 96 ** -0.5, self.yT, s0, h)
        return self.yT

    def mixerC(self, p, l, hT):
        kb = self.kb
        I = self.I
        T = p.T
        nch = p.nch
        pn = p.name
        gall = kb.sb("gall", [64, nch, 16], F32)
        ball = kb.sb("ball", [64, nch, 16], F32)
        qkvD = self.qkvD[pn]
        qtiles = self.qkvTile[pn]
        with self.phase():
            self.alloc_wring()
            cw = kb.sb("cw", [128, 12, 3], F32)
            for k in range(3):
                kb.dma("sp", cw[:, :, k], self.d(I["c_conv"][l, k, :].rearrange("(c q) -> q c", q=128)), allow_slow_non_contiguous=True)
            stage = kb.sb("stage", [128, 12, 512], BF16)
            raw = kb.sb("raw", [128, 514], F32)
            acc = kb.sb("acc", [128, 512], F32)
            av = kb.sb("av", [128, 512], F32)
            sqb = kb.sb("sqb", [128, 512], BF16)
            rs = kb.sb("rsc", [128, 512], F32)
            wt = [self.wload(*self.win(l, "CQKV", i * 512, 512)) for i in range(3)]
            for bi, (s, t0, n) in enumerate(p.blocks):
                s0, Ls = p.seqs[s]
                has_l = t0 > s0
                has_r = t0 + n < s0 + Ls
                for c in range(12):
                    w = wt[c // 4]
                    col = (c % 4) * 128
                    pr = self.pbank("g")
                    self.pfm(pr[:, 0:n], w, col, 128, hT, 8, t0, n)
                    if has_l or has_r:
                        ph = self.pbank("g")
                    if has_l:
                        self.pfm(ph[:, 0:1], w, col, 128, hT, 8, t0 - 1, 1)
                    if has_r:
                        self.pfm(ph[:, 2:3], w, col, 128, hT, 8, t0 + n, 1)
                    kb.copy(raw[:, 1:n + 1], pr[:, 0:n], eng="act")
                    if has_l:
                        kb.copy(raw[:, 0:1], ph[:, 0:1])
                    else:
                        kb.memset(raw[:, 0:1], 0.0, eng="dve")
                    if has_r:
                        kb.copy(raw[:, n + 1:n + 2], ph[:, 2:3])
                    else:
                        kb.memset(raw[:, n + 1:n + 2], 0.0, eng="dve")
                    kb.ts(acc[:, 0:n], raw[:, 1:n + 1], cw[:, c, 1:2], ALU.mult)
                    kb.stt(acc[:, 0:n], raw[:, 0:n], cw[:, c, 0:1], acc[:, 0:n], ALU.mult, ALU.add)
                    kb.stt(acc[:, 0:n], raw[:, 2:n + 2], cw[:, c, 2:3], acc[:, 0:n], ALU.mult, ALU.add)
                    if c < 8:
                        kb.act(av[:, 0:n], acc[:, 0:n], AF.Silu)
                        kb.act(sqb[:, 0:n], av[:, 0:n], AF.Square)
                        pss = self.pbank("g")
                        kb.mm(pss[:, 0:n], self.blk2_bf.ap(), sqb[:, 0:n])
                        kb.act(rs[:, 0:n], pss[:, 0:n], AF.Sqrt, bias=EPS)
                        kb.recip(rs[:, 0:n], rs[:, 0:n])
                        if c < 4:
                            kb.stt(stage[:, c, 0:n], av[:, 0:n], 0.125, rs[:, 0:n], ALU.mult, ALU.mult)
                        else:
                            kb.tt(stage[:, c, 0:n], av[:, 0:n], rs[:, 0:n], ALU.mult)
                    else:
                        kb.act(stage[:, c, 0:n], acc[:, 0:n], AF.Silu)
                kb.dma("sp", V(qtiles[bi], qkvD[:, :, t0:t0 + n]), stage[:, :, 0:n])
            wab = self.wload(*self.win(l, "CAB"))
            abraw = kb.sb("abraw", [64, nch, 32], F32)
            for ch0 in range(0, nch, 16):
                m = min(16, nch - ch0)
                pa = self.pbank("g")
                for j in range(m):
                    ch = ch0 + j
                    for kc in range(8):
                        kb.mm(pa[0:64, j * 32:(j + 1) * 32], hT[:, kc, ch * 64:(ch + 1) * 64], wab[:, kc, 0:32],
                              start=(kc == 0), stop=(kc == 7), skip_group_check=True)
                kb.copy(abraw[:, ch0:ch0 + m, :], pa[0:64, 0:m * 32].re("p (c k) -> p c k", k=32))
            dtb = kb.sb("dtb", [64, 16], F32)
            negA = kb.sb("negA", [64, 16], F32)
            kb.dma("sp", dtb.ap(), self.d(I["c_dt"][l:l + 1, :].to_broadcast([64, 16])))
            kb.dma("sp", negA.ap(), self.d(I["c_alog"][l:l + 1, :].to_broadcast([64, 16])))
            kb.act(negA.ap(), negA.ap(), AF.Exp)
            kb.ts(negA.ap(), negA.ap(), -1.0, ALU.mult)
            xa = kb.sb("xa", [64, nch, 16], F32)
            kb.tt(xa.ap(), abraw[:, :, 0:16], dtb.ap().re("p (o k) -> p o k", o=1).bc([64, nch, 16]), ALU.add)
            kb.act(xa.ap(), xa.ap(), AF.Exp)
            kb.act(xa.ap(), xa.ap(), AF.Ln, bias=1.0)
            kb.tt(gall.ap(), xa.ap(), negA.ap().re("p (o k) -> p o k", o=1).bc([64, nch, 16]), ALU.mult)
            kb.act(xa.ap(), abraw[:, :, 16:32], AF.Exp, scale=-1.0)
            kb.ts(xa.ap(), xa.ap(), 1.0, ALU.add)
            kb.recip(ball.ap(), xa.ap())
        with self.phase():
            self.scan(p, l, gall, ball)
        if self.cfg.get("cphase", 3) < 3:
            return yT
        yT = kb.sb("yTc", [128, 4, T], BF16)
        with self.phase():
            self.alloc_wring()
            wz = self.wload(*self.win(l, "CZ"))
            gon = kb.sb("gon", [128, 1], F32)
            for hf in range(2):
                kb.dma("sp", gon[hf * 64:(hf + 1) * 64, :], self.d(I["c_onorm"][l, :].rearrange("(d o) -> d o", o=1)), allow_slow_non_contiguous=True)
            of_r = [kb.sb("of%d" % i, [128, 512], F32) for i in range(2)]
            ob_r = [kb.sb("ob%d" % i, [128, 512], F32) for i in range(2)]
            os_r = [kb.sb("os%d" % i, [128, 512], F32) for i in range(2)]
            sq_r = [kb.sb("osq%d" % i, [128, 512], F32) for i in range(1)]
            on_r = [kb.sb("on%d" % i, [128, 512], BF16) for i in range(2)]
            sz_r = [kb.sb("sz%d" % i, [128, 128], F32) for i in range(2)]
            for i in range(T // 128):
                of = self.ring("of", of_r); ob = self.ring("ob", ob_r)
                for hf in range(2):
                    ch = 2 * i + hf
                    kb.dma("sp", of[hf * 64:(hf + 1) * 64, :], V(self.oTile[(pn, 0)][ch], self.oD[(pn, 0)][ch]))
                    kb.dma("sp", ob[hf * 64:(hf + 1) * 64, :], V(self.oTile[(pn, 1)][ch], self.oD[(pn, 1)][ch]))
                osum = self.ring("os", os_r)
                kb.tt(osum.ap(), of.ap(), ob.ap(), ALU.add, eng="pool")
                sq = self.ring("osq", sq_r)
                kb.act(sq.ap(), osum.ap(), AF.Square)
                sm = self.ring("sm", self.small)
                kb.reduce(sm[:, 0:8], sq.ap().re("p (h f) -> p h f", h=8))
                sm2 = self.ring("sm", self.small)
                kb.act(sm2[:, 0:8], sm[:, 0:8], AF.Sqrt, scale=1.0 / 64, bias=EPS)
                kb.recip(sm2[:, 0:8], sm2[:, 0:8])
                on = self.ring("on", on_r)
                kb.tt(on.ap().re("p (h f) -> p h f", h=8), osum.ap().re("p (h f) -> p h f", h=8),
                      sm2[:, 0:8].re("p (h f) -> p h f", f=1).bc([128, 8, 64]), ALU.mult)
                pb = self.bfv(self.pbank("t"))
                for c in range(4):
                    kb.tr(pb[:, c * 128:(c + 1) * 128], on[:, c * 128:(c + 1) * 128], self.ident_bf.ap())
                for c in range(4):
                    pz = self.pbank("g")
                    self.pfm(pz[:, 0:128], wz, c * 128, 128, hT, 8, i * 128, 128)
                    sz = self.ring("sz", sz_r)
                    kb.act(sz.ap(), pz[:, 0:128], AF.Silu)
                    kb.stt(yT[:, c, i * 128:(i + 1) * 128], pb[:, c * 128:(c + 1) * 128], gon[:, 0:1], sz.ap(), ALU.mult, ALU.mult)
        return yT

    def scan(self, p, l, gall, ball):
        kb = self.kb
        I = self.I
        pn = p.name
        qkvD = self.qkvD[pn]
        qtiles = self.qkvTile[pn]
        blen = p.blocks[0][2]
        r32 = lambda v: v.bitcast(F32R)
        h3 = lambda v: v.re("p (h f) -> p h f", h=8)
        hs = lambda h: slice(h * 64, (h + 1) * 64)
        o_i8 = CONST["I8"][0]
        dk = kb.sb("dk", [64, NCONST - o_i8], F32)
        kb.dma("sp", dk.ap(), self.d(I["consts"][0:64, o_i8:NCONST]))

        def cd(n):
            o, w = CONST[n]
            return dk[:, o - o_i8:o - o_i8 + w]
        ones64 = self.cv("ones", 64)[:, 0:64]
        identf = self.cv("ident", 64)[:, 0:64]
        cR = kb.sb("cR", [64, 4, 64], F32)
        for i_, src_ in enumerate((ones64, cd("nUf"), cd("nUb"), cd("noff"))):
            kb.copy(r32(cR[:, i_, :]), src_)
        onesR = r32(cR[:, 0, :])
        noff = r32(cR[:, 3, :])
        I8 = cd("I8")
        MSv = lambda lv: cd("MS")[:, lv * 64:(lv + 1) * 64].re("p (o f) -> p o f", o=1).bc([64, 8, 64])
        Sstage = kb.sb("Sstage", [64, 8, 64], F32)
        mask_eng = self.cfg.get("mask_eng", "pool")

        def stream(d):
            u = lambda nm, w=512: kb.sb("s%d%s" % (d, nm), [64, w], F32)
            R = u("R", 1024); E = u("E"); Rb = u("Rb"); tmpA = u("tmpA"); Q = u("Q"); QT = u("QT")
            Dr = [u("D0"), u("D1")]; DTr = [u("DT0"), u("DT1")]
            Nq = u("Nq"); NqT = u("NqT"); Y = u("Y"); Y2 = u("Y2"); QKT = u("QKT")
            S = kb.sb("s%dS" % d, [64, 8, 64], F32)
            dkd = kb.sb("s%ddkd" % d, [64, 1536], F32)
            dsm = [kb.sb("s%ddsm%d" % (d, i), [64, 48], F32) for i in range(3)]
            qk_ring = [kb.sb("s%dqk%d" % (d, i), [128, 12, 64], BF16) for i in range(2)]
            qk0_ring = [kb.sb("s%dqk0_%d" % (d, i), [64, 16, 64], BF16) for i in range(2)]
            qf = kb.sb("s%dqf" % d, [64, 8, 64], F32)
            gp = "g%d" % d
            tp = "t%d" % d
            oc, _ = CONST["CMf" if d == 0 else "CMb"]
            on_, _ = CONST["NEGMf" if d == 0 else "NEGMb"]
            kb.dma("sp", dkd[:, 0:1024], self.d(I["consts"][0:64, oc:oc + 1024]))
            kb.dma("sp", dkd[:, 1024:1536], self.d(I["consts"][0:64, on_:on_ + 512]))
            CMd = dkd[:, 0:1024]
            NEGMd = dkd[:, 1024:1536]
            U = cd("Uf" if d == 0 else "Ub")
            nU = r32(cR[:, 1 + d, :])
            rq = [0]

            def load_qk(ch):
                t = qk_ring[rq[0] % 2]
                t0_ = qk0_ring[rq[0] % 2]
                rq[0] += 1
                qt = qtiles[(ch * 64) // blen]
                kb.dma("sp", t.ap(), V(qt, qkvD[:, :, ch * 64:(ch + 1) * 64]))
                for hf in range(2):
                    kb.dma("sp", t0_.ap().re("p (c two) t -> p c two t", two=2)[:, :, hf, :],
                           V(qt, qkvD[hf * 64:(hf + 1) * 64, 0:8, ch * 64:(ch + 1) * 64]))
                return t, t0_
            dsi = [0]
            for s, (s0, Ls) in enumerate(p.seqs):
                if p.ncache:
                    kb.dma("sp", Sstage.ap(), self.d(I["sf" if d == 0 else "sb"][l].rearrange("h d v -> d h v")))
                else:
                    kb.memset(Sstage.ap(), 0.0, eng="dve")
                kb.copy(r32(S.ap()), Sstage.ap())
                chs = list(range(s0 // 64, (s0 + Ls) // 64))
                if d == 1:
                    chs = chs[::-1]
                qk_next = load_qk(chs[0])
                for ci, ch in enumerate(chs):
                    qk, qk0 = qk_next
                    if ci + 1 < len(chs):
                        qk_next = load_qk(chs[ci + 1])
                    g = gall[:, ch, d * 8:(d + 1) * 8]
                    b = ball[:, ch, d * 8:(d + 1) * 8]
                    Kt = lambda h: qk0[:, 8 + h, :]
                    Qt = lambda h: qk0[:, h, :]
                    pg = self.pbank(gp)
                    kb.mm(pg[0:64, 0:8], U, g, skip_group_check=True)
                    kb.mm(pg[0:64, 8:16], ones64, g, skip_group_check=True)
                    sm = dsm[dsi[0] % 3]
                    dsi[0] += 1
                    gcs = sm[:, 0:8]; egc = sm[:, 8:16]; egl = sm[:, 16:24]; ek = sm[:, 24:32]; bg = sm[:, 32:40]
                    kb.copy(gcs, pg[0:64, 0:8])
                    kb.act(egc, gcs, AF.Exp)
                    kb.act(egl, pg[0:64, 8:16], AF.Exp)
                    kb.tt(ek, pg[0:64, 8:16], gcs, ALU.subtract)
                    kb.act(ek, ek, AF.Exp)
                    kb.tt(bg, b, egc, ALU.mult)
                    yield
                    kb.tt(r32(R.ap()).re("p (o h f) -> p o h f", o=2, h=8), CMd.re("p (o h f) -> p o h f", o=2, h=8),
                          g.re("p (o h f) -> p o h f", o=1, f=1).bc([64, 2, 8, 64]), ALU.mult)
                    pdf = self.pbank(gp)
                    kb.mm(pdf[0:64, :], onesR, r32(R[:, 0:512]), start=True, stop=False)
                    kb.mm(pdf[0:64, :], nU, r32(R[:, 512:1024]), start=False, stop=True)
                    kb.stt(E.ap(), pdf[0:64, :], 0.0, NEGMd, ALU.min, ALU.add)
                    kb.act(E.ap(), E.ap(), AF.Exp)
                    yield
                    kb.tt(h3(r32(Rb.ap())), h3(I8), b.re("p (h f) -> p h f", f=1).bc([64, 8, 64]), ALU.mult)
                    pbr = self.pbank(gp)
                    kb.mm(pbr[0:64, :], noff, r32(Rb.ap()))
                    pG = self.pbank(gp)
                    for h in range(8):
                        kb.mm(pG[0:64, hs(h)], Kt(h), Kt(h), skip_group_check=True)
                    yield
                    kb.tt(tmpA.ap(), pG[0:64, :], E.ap(), ALU.mult)
                    kb.tt(Q.ap(), tmpA.ap(), pbr[0:64, :], ALU.mult)
                    pKQ = self.pbank(gp)
                    for h in range(8):
                        kb.mm(pKQ[0:64, hs(h)], Kt(h), Qt(h), skip_group_check=True)
                    kb.tt(r32(QKT.ap()), pKQ[0:64, :], E.ap(), ALU.mult)
                    yield
                    pT = self.pbank(gp)
                    for h in range(8):
                        kb.tr(pT[0:64, hs(h)], Q[:, hs(h)], identf)
                    kb.copy(QT.ap(), pT[0:64, :], eng="act")
                    yield
                    D = Dr[0]; DT = DTr[0]
                    kb.tt(h3(tmpA.ap()), h3(Q.ap()), MSv(0), ALU.mult, eng=mask_eng)
                    kb.tt(r32(D.ap()), tmpA.ap(), I8, ALU.add)
                    kb.tt(h3(E.ap()), h3(QT.ap()), MSv(0), ALU.mult, eng=mask_eng)
                    kb.tt(r32(DT.ap()), E.ap(), I8, ALU.add)
                    yield
                    for lv in range(1, 6):
                        kb.tt(h3(r32(NqT.ap())), h3(QT.ap()), MSv(lv), ALU.mult, eng=mask_eng)
                        if lv < 5:
                            kb.tt(h3(r32(Nq.ap())), h3(Q.ap()), MSv(lv), ALU.mult, eng=mask_eng)
                        pY = self.pbank(gp)
                        for h in range(8):
                            kb.mm(pY[0:64, hs(h)], r32(NqT[:, hs(h)]), r32(D[:, hs(h)]), skip_group_check=True)
                        kb.tt(r32(Y.ap()), pY[0:64, :], I8, ALU.add)
                        if lv < 5:
                            pY2 = self.pbank(gp)
                            for h in range(8):
                                kb.mm(pY2[0:64, hs(h)], r32(Nq[:, hs(h)]), r32(DT[:, hs(h)]), skip_group_check=True)
                            kb.tt(r32(Y2.ap()), pY2[0:64, :], I8, ALU.add)
                        yield
                        pZ = self.pbank(gp)
                        for h in range(8):
                            kb.mm(pZ[0:64, hs(h)], r32(DT[:, hs(h)]), r32(Y[:, hs(h)]), skip_group_check=True)
                        Dn = Dr[lv % 2]
                        kb.copy(r32(Dn.ap()), pZ[0:64, :], eng="act")
                        if lv < 5:
                            pZ2 = self.pbank(gp)
                            for h in range(8):
                                kb.mm(pZ2[0:64, hs(h)], r32(D[:, hs(h)]), r32(Y2[:, hs(h)]), skip_group_check=True)
                            DTn = DTr[lv % 2]
                            kb.copy(r32(DTn.ap()), pZ2[0:64, :], eng="act")
                            DT = DTn
                        D = Dn
                        yield
                    P = D
                    Ktok = Q; Vtok = QT; Vb = Nq; Kbg = NqT; Kt2 = Y; wT = Y2
                    vnew = DTr[0]; o = DTr[1]; u_ = E; o1 = tmpA
                    pb = self.bfv(self.pbank(tp))
                    for c in range(4):
                        kb.tr(pb[0:64, c * 128:(c + 1) * 128], qk[:, 4 + c, :], self.ident_bf.ap())
                    kb.copy(Ktok.ap(), pb[0:64, 0:512], eng="act")
                    kb.tt(h3(r32(Kbg.ap())), h3(Ktok.ap()), bg.re("p (h f) -> p h f", f=1).bc([64, 8, 64]), ALU.mult)
                    kb.tt(h3(r32(Kt2.ap())), h3(Ktok.ap()), ek.re("p (h f) -> p h f", f=1).bc([64, 8, 64]), ALU.mult)
                    yield
                    pb2 = self.bfv(self.pbank(tp))
                    for c in range(4):
                        kb.tr(pb2[0:64, c * 128:(c + 1) * 128], qk[:, 8 + c, :], self.ident_bf.ap())
                    kb.copy(Vtok.ap(), pb2[0:64, 0:512], eng="act")
                    kb.tt(h3(r32(Vb.ap())), h3(Vtok.ap()), b.re("p (h f) -> p h f", f=1).bc([64, 8, 64]), ALU.mult)
                    yield
                    pu = self.pbank(gp)
                    for h in range(8):
                        kb.mm(pu[0:64, hs(h)], r32(P[:, hs(h)]), r32(Vb[:, hs(h)]), skip_group_check=True)
                    kb.copy(u_.ap(), pu[0:64, :], eng="act")
                    pw = self.pbank(gp)
                    for h in range(8):
                        kb.mm(pw[0:64, hs(h)], r32(Kbg[:, hs(h)]), r32(P[:, hs(h)]), skip_group_check=True)
                    kb.copy(r32(wT.ap()), pw[0:64, :])
                    kb.copy(r32(qf.ap()), qk0[:, 0:8, :], eng="act")
                    yield
                    pws = self.pbank(gp)
                    for h in range(8):
                        kb.mm(pws[0:64, hs(h)], r32(wT[:, hs(h)]), r32(S[:, h, :]), skip_group_check=True)
                    kb.tt(r32(vnew.ap()), u_.ap(), pws[0:64, :], ALU.subtract)
                    poa = self.pbank(gp)
                    for h in range(8):
                        kb.mm(poa[0:64, hs(h)], r32(qf[:, h, :]), r32(S[:, h, :]), skip_group_check=True)
                    kb.tt(h3(o1.ap()), h3(poa[0:64, :]), egc.re("p (h f) -> p h f", f=1).bc([64, 8, 64]), ALU.mult)
                    yield
                    pob = self.pbank(gp)
                    for h in range(8):
                        kb.mm(pob[0:64, hs(h)], r32(QKT[:, hs(h)]), r32(vnew[:, hs(h)]), skip_group_check=True)
                    kb.tt(r32(o.ap()), o1.ap(), pob[0:64, :], ALU.add)
                    kb.dma("sp", V(self.oTile[(pn, d)][ch], self.oD[(pn, d)][ch]), o.ap())
                    pS = self.pbank(gp)
                    for h in range(8):
                        kb.mm(pS[0:64, hs(h)], r32(Kt2[:, hs(h)]), r32(vnew[:, hs(h)]), skip_group_check=True)
                    kb.tt(r32(S.ap()), S.ap(), egl.re("p (h f) -> p h f", f=1).bc([64, 8, 64]), ALU.mult)
                    kb.tt(r32(S.ap()), S.ap(), pS[0:64, :].re("p (h f) -> p h f", h=8), ALU.add)
                    yield
                if pn == "C":
                    kb.dma("sp", self.d(self.O["nsf" if d == 0 else "nsb"][s, l].rearrange("h d v -> d h v")), S.ap(), is_output=True)
        ns = self.cfg.get("nstreams", 2)
        if ns == 2:
            gens = [stream(0), stream(1)]
        else:
            def seq():
                yield from stream(0)
                yield from stream(1)
            gens = [seq()]
        while gens:
            for g_ in list(gens):
                try:
                    next(g_)
                except StopIteration:
                    gens.remove(g_)

    def merge(self, p, l, hT, yT, merged, wname, xi):
        kb = self.kb
        sg_ring = [kb.sb("sg%d" % i, [128, 512], F32) for i in range(1)]
        pr_ring = [kb.sb("pr%d" % i, [128, 512], BF16) for i in range(1)]
        for half in range(2):
            wp = self.wload(self.I[wname][l, :, half * 512:(half + 1) * 512], 512, 512)
            wg = self.wload(*self.win(l, "G", xi * D + half * 512, 512))
            for (s, t0, n) in p.blocks:
                for c in range(4):
                    pa = self.pbank("g"); pg = self.pbank("g")
                    self.pfm(pa[:, 0:n], wp, c * 128, 128, yT, 4, t0, n)
                    self.pfm(pg[:, 0:n], wg, c * 128, 128, hT, 8, t0, n)
                    sg = self.ring("sg", sg_ring)
                    kb.act(sg[:, 0:n], pg[:, 0:n], AF.Sigmoid)
                    dst = merged[:, half * 4 + c, t0:t0 + n]
                    if xi == self.first_mixer:
                        kb.tt(dst, pa[:, 0:n], sg[:, 0:n], ALU.mult)
                    else:
                        pr = self.ring("pr", pr_ring)
                        kb.tt(pr[:, 0:n], pa[:, 0:n], sg[:, 0:n], ALU.mult)
                        kb.tt(dst, dst, pr[:, 0:n], ALU.add, eng="pool")

    def tail(self, p, l, merged, AB):
        kb = self.kb
        I = self.I
        last = (l == L - 1)
        gb = self.make_gb(p, l)
        if last:
            self.fnb = kb.sb("fnb", [128, D], F32)
            kb.dma("sp", self.fnb.ap(), self.d(self.I["final_norm"][None, :].to_broadcast([128, D])))
        wout = [kb.sb("wout%d" % i, [128, 8, 512], BF16) for i in range(2)]
        for h in range(2):
            kb.dma("pool", wout[h].ap(), self.d(I["w_out"][l, :, h * 512:(h + 1) * 512].rearrange("(kc q) n -> q kc n", q=128)))
        h2T = kb.sb("h2T", [128, 8, 512], BF16)
        actT = kb.sb("actT", [128, 22, 512], BF16)
        wd = kb.sb("wd", [128, 22, 512], BF16)
        xm = [kb.sb("xm%d" % i, [128, D], F32) for i in range(4)]
        tmp_ring = [kb.sb("tmp%d" % i, [128, 512], F32) for i in range(2)]
        sgb_ring = [kb.sb("sgb%d" % i, [128, 512], BF16) for i in range(2)]
        yo_ring = [kb.sb("yo%d" % i, [128, D], F32) for i in range(2)]
        fspecs = []
        for f0 in range(0, DFF, 512):
            fn = min(512, DFF - f0)
            fspecs.append((I["w_gate"][l, :, f0:f0 + fn], D, fn))
            fspecs.append((I["w_up"][l, :, f0:f0 + fn], D, fn))
        for b0 in range(0, p.T, 512):
            it = self.wstream(fspecs)
            pre = [next(it), next(it)]
            for i in range(4):
                ti = b0 // 128 + i
                xt = self.ring("xt", self.xt_ring)
                kb.dma("sp", xt.ap(), self.xsrc(p, l, ti))
                for h in range(2):
                    pm = self.pbank("g")
                    for kc in range(8):
                        kb.mm(pm.ap(), merged[:, kc, ti * 128:(ti + 1) * 128], wout[h][:, kc, :], start=(kc == 0), stop=(kc == 7))
                    tmp = self.ring("tmp", tmp_ring)
                    kb.tt(tmp.ap(), pm.ap(), gb[:, 0, h * 512:(h + 1) * 512], ALU.mult)
                    kb.tt(xm[i][:, h * 512:(h + 1) * 512], xt[:, h * 512:(h + 1) * 512], tmp.ap(), ALU.add, eng="pool")
                self.norm_one(xm[i], AB[:, 16:24], AB[:, 24:32], h2T, i * 128)
            for f0 in range(0, DFF, 512):
                fn = min(512, DFF - f0)
                if f0 == 0:
                    wg, wu = pre
                else:
                    wg = next(it); wu = next(it)
                for fc in range(fn // 128):
                    pg = self.pbank("g"); pu = self.pbank("g")
                    self.pfm(pg.ap(), wg, fc * 128, 128, h2T, 8, 0, 512)
                    self.pfm(pu.ap(), wu, fc * 128, 128, h2T, 8, 0, 512)
                    sg = self.ring("sgb", sgb_ring)
                    kb.act(sg.ap(), pg.ap(), AF.Silu)
                    kb.tt(actT[:, f0 // 128 + fc, :], pu.ap(), sg.ap(), ALU.mult)
            for h in range(2):
                for k0 in range(0, 22, 8):
                    kn = min(8, 22 - k0)
                    kb.dma("pool", wd[:, k0:k0 + kn, :], self.d(I["w_down"][l, k0 * 128:(k0 + kn) * 128, h * 512:(h + 1) * 512].rearrange("(kc q) n -> q kc n", q=128)))
                for i in range(4):
                    pd = self.pbank("g")
                    for kc in range(22):
                        kb.mm(pd.ap(), actT[:, kc, i * 128:(i + 1) * 128], wd[:, kc, :], start=(kc == 0), stop=(kc == 21))
                    tmp = self.ring("tmp", tmp_ring)
                    kb.tt(tmp.ap(), pd.ap(), gb[:, 1, h * 512:(h + 1) * 512], ALU.mult)
                    kb.tt(xm[i][:, h * 512:(h + 1) * 512], xm[i][:, h * 512:(h + 1) * 512], tmp.ap(), ALU.add, eng="pool")
            for i in range(4):
                ti = b0 // 128 + i
                if not last:
                    kb.dma("sp", V(self.xT[p.name][ti], self.xbuf[p.name][ti * 128:(ti + 1) * 128, :]), xm[i].ap())
                else:
                    sm = self.ring("sm", self.small)
                    kb.act(self.junk_bf.ap(), xm[i].ap(), AF.Square, accum_out=sm[:, 0:1])
                    kb.act(sm[:, 1:2], sm[:, 0:1], AF.Sqrt, scale=1.0 / D, bias=EPS)
                    kb.recip(sm[:, 2:3], sm[:, 1:2])
                    yo = self.ring("yo", yo_ring)
                    kb.stt(yo.ap(), xm[i].ap(), sm[:, 2:3], self.fnb.ap(), ALU.mult, ALU.mult)
                    kb.dma("sp", self.d(self.O["ys" if p.name == "S" else "yc"][ti * 128:(ti + 1) * 128, :]), yo.ap(), is_output=True)

    def layer(self, p, l):
        kb = self.kb
        mixers = self.cfg.get("mixers", "ABC")
        self.first_mixer = {"A": 0, "B": 1, "C": 2}[mixers[0]]
        with self.phase():
            AB = self.layer_consts(p, l)
            merged = kb.sb("merged", [128, 8, p.T], BF16)
            with self.phase():
                hT = kb.sb("hT", [128, 8, p.T], BF16)
                for i in range(p.T // 128):
                    xt = self.ring("xt", self.xt_ring)
                    kb.dma("sp", xt.ap(), self.xsrc(p, l, i))
                    self.norm_one(xt, AB[:, 0:8], AB[:, 8:16], hT, i * 128)
                self.stage(2)
                if "A" in mixers:
                    with self.phase():
                        self.alloc_wring()
                        yT = self.mixerA(p, l, hT)
                        self.stage(5)
                        self.merge(p, l, hT, yT, merged, "w_pa", 0)
                        self.stage(6)
                if "B" in mixers:
                    with self.phase():
                        self.alloc_wring()
                        yT = self.mixerB(p, l, hT)
                        self.merge(p, l, hT, yT, merged, "w_pb", 1)
                if "C" in mixers:
                    with self.phase():
                        yT = self.mixerC(p, l, hT)
                        with self.phase():
                            self.alloc_wring()
                            self.merge(p, l, hT, yT, merged, "w_pc", 2)
            with self.phase():
                self.alloc_wring()
                self.tail(p, l, merged, AB)

    def build(self):
        with ExitStack() as gst, ExitStack() as st:
            kb = self.kb = KB(self.nc, st, gst)
            self.modT = Tile("modT", None)
            self.qkvTile = {pn: [Tile("qkv%s%d" % (pn, i), None) for i in range(8)] for pn in ("S", "C")}
            self.oTile = {(pn, d): [Tile("o%s%d_%d" % (pn, d, i), None) for i in range(T // 64)]
                          for pn, T in (("S", TS), ("C", TC)) for d in (0, 1)}
            self.xT = {pn: [Tile("x%s%d" % (pn, i), None) for i in range(T // 128)] for pn, T in (("S", TS), ("C", TC))}
            try:
                self.setup()
                self.mods()
                self.stage(1)
                for pn in self.cfg.get("passes", "CS"):
                    p = PassCfg(pn)
                    for l in range(self.cfg.get("layers", L)):
                        self.layer(p, l)
            except _Stop:
                pass
            kb.finish()
            print("ninst", kb.ninst, "nwait", kb.nwait, "nsem", kb.nsem)
        return self.nc


def _core_inputs(core, inp, shared):
    b = core % 4
    m = dict(shared)
    m["xs"] = np.ascontiguousarray(inp["x_sample"][b])
    m["xc"] = np.ascontiguousarray(inp["x_prompt"][2 * core:2 * core + 2].reshape(TC, D))
    m["cka"] = np.ascontiguousarray(inp["cache_ka"][b].reshape(L, PAST, 128))
    m["cva"] = np.ascontiguousarray(inp["cache_va"][b].reshape(L, PAST, 128))
    m["cckv"] = np.ascontiguousarray(inp["cache_ckv"][b])
    m["ckpe"] = np.ascontiguousarray(inp["cache_kpe"][b])
    m["sf"] = np.ascontiguousarray(inp["state_fwd"][b])
    m["sb"] = np.ascontiguousarray(inp["state_bwd"][b])
    m["cv2"] = np.ascontiguousarray(np.stack([inp["c"][b], inp["c_ctx"]], 0))
    return m


def _shared_inputs(inp):
    f = lambda a: np.ascontiguousarray(np.asarray(a, dtype=np.float32))
    sh = {}
    for n in ("w_mod", "b_mod", "norm1", "norm2", "b_qnorm", "b_kvnorm", "w_uq", "c_onorm", "w_pa", "w_pb", "w_pc",
              "w_out", "w_gate", "w_up", "w_down", "final_norm"):
        sh[n] = f(inp[n])
    sh["w_in_r"] = _prep_w_in(f(inp["w_in"]))
    sh["aq"] = f(np.stack([inp["a_qnorm"], inp["a_qnorm"][:, PERM64]], 1))
    sh["akn"] = f(np.stack([inp["a_knorm"], inp["a_knorm"][:, PERM64]], 1))
    wuq = f(inp["w_uq"]).reshape(L, 384, 8, 96)
    wuqs = np.zeros_like(wuq)
    wuqs[..., 64:96] = wuq[..., 64:96][..., PERM32]
    sh["w_uqs"] = np.ascontiguousarray(wuqs.reshape(L, 384, 768))
    wukv = f(inp["w_ukv"]).reshape(L, 256, 8, 128)
    sh["w_ukv_k"] = np.ascontiguousarray(wukv[..., 0:64].reshape(L, 256, 512))
    sh["w_ukv_v"] = np.ascontiguousarray(wukv[..., 64:128].reshape(L, 256, 512))
    sh["c_conv"] = f(inp["c_conv"]).reshape(L, 3, 1536)
    sh["c_alog"] = f(inp["c_alog"]).reshape(L, 16)
    sh["c_dt"] = f(inp["c_dt_bias"]).reshape(L, 16)
    tA, tB = _tables(TS, True)
    sh["tabA_S"], sh["tabB_S"] = tA, tB
    tA, tB = _tables(TC, False)
    sh["tabA_C"], sh["tabB_C"] = tA, tB
    sh["consts"] = _consts()
    return sh


def _run(inp, cfg=None):
    inp = {k: np.asarray(v) for k, v in inp.items()}
    b = Builder(cfg)
    nc = b.build()
    shared = _shared_inputs(inp)
    ncores = (cfg or {}).get("ncores", 8)
    in_maps = [_core_inputs(c, inp, shared) for c in range(ncores)]
    res = run_bass_kernel_spmd(nc, in_maps, core_ids=list(range(ncores)))
    return res.results


def kernel(**inputs):
    r = _run(inputs)
    y_prompt = np.concatenate([r[c]["yc"].reshape(2, 256, D) for c in range(8)], 0)
    y_sample = np.stack([r[c]["ys"] for c in range(4)], 0)
    def cat(n, shp):
        return np.concatenate([r[c][n].reshape((2, L, 256) + shp) for c in range(8)], 0)
    new_ka = cat("nka", (2, 64)); new_va = cat("nva", (2, 64))
    new_ckv = cat("nckv", (256,)); new_kpe = cat("nkpe", (32,))
    new_sf = np.concatenate([r[c]["nsf"] for c in range(8)], 0)
    new_sb = np.concatenate([r[c]["nsb"] for c in range(8)], 0)
    return tuple(np.ascontiguousarray(a.astype(np.float32)) for a in
                 (y_prompt, y_sample, new_ka, new_va, new_ckv, new_kpe, new_sf, new_sb))
```

```python
import numpy as np
import concourse.bass as bass
import concourse.mybir as mybir
from contextlib import ExitStack

F32 = mybir.dt.float32
BF16 = mybir.dt.bfloat16
F32R = mybir.dt.float32r
AF = mybir.ActivationFunctionType
ALU = mybir.AluOpType
AX = mybir.AxisListType


class Tile:
    def __init__(self, name, t):
        self.name = name
        self.t = t
        self.w = None
        self.r = {}

    def __getitem__(self, idx):
        return V(self, self.t[idx])

    def ap(self):
        return V(self, self.t[:])


class V:
    def __init__(self, tile, ap):
        self.tile = tile
        self.a = ap

    def __getitem__(self, idx):
        return V(self.tile, self.a[idx])

    def re(self, pat, **kw):
        return V(self.tile, self.a.rearrange(pat, **kw))

    def bc(self, shape):
        return V(self.tile, self.a.to_broadcast(shape))

    def bitcast(self, dt):
        return V(self.tile, self.a.bitcast(dt))


def _ap(x):
    return x.a if isinstance(x, V) else x


class KB:
    EPOCH = 30000
    NDSEM = 16
    SAME_ENGINE_SYNC = True

    def __init__(self, nc, st, gst=None):
        self.nc = nc
        self.st = st
        self.gst = gst if gst is not None else st
        self.eng = dict(pe=nc.tensor, act=nc.scalar, dve=nc.vector, pool=nc.gpsimd, sp=nc.sync)
        self.esem = {}
        self.ecnt = {}
        self.nsem = 0
        for e in self.eng:
            self.esem[e] = self._new_sem()
            self.ecnt[e] = 0
        self.waited = {}
        self.dsem = [self._new_sem() for _ in range(self.NDSEM)]
        self.dval = [0] * self.NDSEM
        self.di = 0
        self.dpi = 0
        self.ninst = 0
        self.nwait = 0
        self.out_tokens = []
        self.psum_banks = []
        self.psum_i = 0

    def _new_sem(self):
        self.nsem += 1
        return self.gst.enter_context(self.nc.semaphore("s%d" % self.nsem))

    def sb(self, name, shape, dt):
        self.nalloc = getattr(self, "nalloc", 0) + 1
        base = name
        name = "%s_%d" % (name, self.nalloc)
        if not hasattr(self, "names"):
            self.names = {}
        self.names[base] = name
        t = self.st.enter_context(self.nc.sbuf_tensor(name, list(shape), dt))
        return Tile(name, t)

    def ps(self, name, shape, dt):
        t = self.st.enter_context(self.nc.psum_tensor(name, list(shape), dt))
        return Tile(name, t)

    def dram(self, name, shape, dt, kind="Internal"):
        t = self.nc.dram_tensor(name, list(shape), dt, kind=kind)
        return Tile(name, t)

    def _wait(self, e, tok):
        if tok is None:
            return
        sem, val = tok
        if (not self.SAME_ENGINE_SYNC or e == "pe") and sem is self.esem[e]:
            return
        key = (e, sem.num)
        if self.waited.get(key, 0) >= val:
            return
        self.eng[e].wait_ge(sem, val)
        self.waited[key] = val
        self.nwait += 1

    def _deps(self, e, reads, writes):
        for t in reads:
            self._wait(e, t.w)
        for t in writes:
            self._wait(e, t.w)
            for tok in t.r.values():
                self._wait(e, tok)

    def _mark(self, key, tok, reads, writes):
        for t in reads:
            t.r[key] = tok
        for t in writes:
            t.w = tok
            t.r = {}

    mute = False

    def op(self, e, fn, reads=(), writes=()):
        if self.mute:
            return None
        reads = [x.tile if isinstance(x, V) else x for x in reads]
        writes = [x.tile if isinstance(x, V) else x for x in writes]
        self._deps(e, reads, writes)
        if self.ecnt[e] >= self.EPOCH:
            self.esem[e] = self._new_sem()
            self.ecnt[e] = 0
        inst = fn(self.eng[e])
        inst.then_inc(self.esem[e], 1)
        self.ecnt[e] += 1
        self.ninst += 1
        tok = (self.esem[e], self.ecnt[e])
        self._mark(e, tok, reads, writes)
        return tok

    def dma(self, q, out, in_, is_output=False, **kw):
        if self.mute:
            return None
        reads = [in_.tile]
        writes = [out.tile]
        self._deps(q, reads, writes)
        half = self.NDSEM // 2
        if q == "pool":
            i = half + self.dpi % half
            self.dpi += 1
        else:
            i = self.di % half
            self.di += 1
        if self.dval[i] > 0:
            self._wait(q, (self.dsem[i], self.dval[i]))
        inst = self.eng[q].dma_start(out=out.a, in_=in_.a, **kw)
        inst.then_inc(self.dsem[i], 16)
        self.dval[i] += 16
        self.ninst += 1
        tok = (self.dsem[i], self.dval[i])
        self._mark("d%d" % i, tok, reads, writes)
        if is_output:
            self.out_tokens.append(tok)
        return tok

    def barrier(self):
        if self.mute:
            return
        toks = [(self.esem[e], self.ecnt[e]) for e in self.eng if self.ecnt[e] > 0]
        toks += [(self.dsem[i], self.dval[i]) for i in range(self.NDSEM) if self.dval[i] > 0]
        for e in self.eng:
            for tok in toks:
                self._wait(e, tok)

    def finish(self):
        if self.mute:
            return
        for tok in self.out_tokens:
            self._wait("sp", tok)
        for i in range(self.NDSEM):
            if self.dval[i] > 0:
                self._wait("sp", (self.dsem[i], self.dval[i]))

    def mm(self, out, lhsT, rhs, start=True, stop=True, **kw):
        return self.op("pe", lambda e: e.matmul(_ap(out), _ap(lhsT), _ap(rhs), start=start, stop=stop, **kw),
                       reads=[lhsT, rhs], writes=[out])

    def tr(self, out, in_, ident):
        return self.op("pe", lambda e: e.transpose(_ap(out), _ap(in_), _ap(ident)),
                       reads=[in_, ident], writes=[out])

    def act(self, out, in_, func, bias=None, scale=None, accum_out=None, eng="act"):
        kw = {}
        rd = [in_]
        wr = [out]
        if bias is not None:
            kw["bias"] = _ap(bias)
            if isinstance(bias, V):
                rd.append(bias)
        if scale is not None:
            kw["scale"] = _ap(scale)
            if isinstance(scale, V):
                rd.append(scale)
        if accum_out is not None:
            kw["accum_out"] = _ap(accum_out)
            wr.append(accum_out)
        return self.op(eng, lambda e: e.activation(_ap(out), _ap(in_), func, **kw), reads=rd, writes=wr)

    def tt(self, out, in0, in1, op, eng="dve"):
        return self.op(eng, lambda e: e.tensor_tensor(_ap(out), _ap(in0), _ap(in1), op),
                       reads=[in0, in1], writes=[out])

    def ts(self, out, in0, s1, op0, s2=None, op1=None, eng="dve", accum_out=None):
        rd = [in0]
        wr = [out]
        if isinstance(s1, V):
            rd.append(s1)
        if isinstance(s2, V):
            rd.append(s2)
        kw = {}
        if op1 is not None:
            kw["op1"] = op1
        if accum_out is not None:
            kw["accum_out"] = _ap(accum_out)
            wr.append(accum_out)
        return self.op(eng, lambda e: e.tensor_scalar(_ap(out), _ap(in0), _ap(s1), _ap(s2), op0, **kw),
                       reads=rd, writes=wr)

    def stt(self, out, in0, scalar, in1, op0, op1, eng="dve"):
        rd = [in0, in1]
        if isinstance(scalar, V):
            rd.append(scalar)
        return self.op(eng, lambda e: e.scalar_tensor_tensor(_ap(out), _ap(in0), _ap(scalar), _ap(in1), op0, op1),
                       reads=rd, writes=[out])

    def copy(self, out, in_, eng="dve"):
        if eng == "act":
            return self.act(out, in_, AF.Copy)
        return self.op(eng, lambda e: e.tensor_copy(_ap(out), _ap(in_)), reads=[in_], writes=[out])

    def memset(self, out, val, eng="pool"):
        return self.op(eng, lambda e: e.memset(_ap(out), val), writes=[out])

    def recip(self, out, in_, eng="dve"):
        return self.op(eng, lambda e: e.reciprocal(_ap(out), _ap(in_)), reads=[in_], writes=[out])

    def reduce(self, out, in_, op=ALU.add, axis=AX.X, eng="dve"):
        return self.op(eng, lambda e: e.tensor_reduce(_ap(out), _ap(in_), axis, op), reads=[in_], writes=[out])


from concourse.bass_utils import run_bass_kernel_spmd

D = 1024; L = 2; DFF = 2816; EPS = 1e-6
TS = 2048; TC = 512; PAST = 512
O_QA, O_KA, O_VA, O_CQ, O_CKV, O_KPE, O_QKV, O_Z, O_A, O_B, O_G = 0, 512, 640, 768, 1152, 1408, 1440, 2976, 3488, 3504, 3520
PERM64 = np.array(list(range(16, 32)) + list(range(0, 16)) + list(range(48, 64)) + list(range(32, 48)))
PERM32 = np.array(list(range(8, 16)) + list(range(0, 8)) + list(range(24, 32)) + list(range(16, 24)))
GRP = {}
_off = 0
for _n, _w in [("AQ", 512), ("AQS", 512), ("AK", 256), ("AKS", 256), ("AV", 128), ("AKT", 128), ("BCQ", 384),
               ("BCKV", 256), ("BKPE1", 96), ("BKPE2", 96), ("BKPET", 32), ("CQKV", 1536), ("CZ", 512), ("CAB", 32),
               ("G", 3072)]:
    GRP[_n] = (_off, _w)
    _off += _w
NCOL = _off


def _prep_w_in(w_in):
    cols = []
    qa = np.arange(O_QA, O_QA + 512)
    cols.append(qa)
    cols.append((qa.reshape(8, 64)[:, PERM64]).reshape(-1))
    ka = np.arange(O_KA, O_KA + 128).reshape(2, 64)
    kdup = np.concatenate([ka[0], ka[0], ka[1], ka[1]])
    cols.append(kdup)
    kas = ka[:, PERM64]
    cols.append(np.concatenate([kas[0], kas[0], kas[1], kas[1]]))
    cols.append(np.arange(O_VA, O_VA + 128))
    cols.append(np.arange(O_KA, O_KA + 128))
    cols.append(np.arange(O_CQ, O_CQ + 384))
    cols.append(np.arange(O_CKV, O_CKV + 256))
    kpe = np.arange(O_KPE, O_KPE + 32)
    Z = -1 * np.ones(64, dtype=np.int64)
    cols.append(np.concatenate([Z, kpe]))
    cols.append(np.concatenate([Z, kpe[PERM32]]))
    cols.append(kpe)
    cols.append(np.arange(O_QKV, O_QKV + 1536))
    cols.append(np.arange(O_Z, O_Z + 512))
    cols.append(np.arange(O_A, O_A + 32))
    cols.append(np.arange(O_G, O_G + 3072))
    idx = np.concatenate(cols)
    assert idx.shape[0] == NCOL
    wz = np.concatenate([w_in, np.zeros(w_in.shape[:2] + (1,), np.float32)], axis=2)
    return np.ascontiguousarray(wz[:, :, idx])


def _tables(T, rope):
    tA = np.zeros((2, 128, T), np.float32)
    tB = np.zeros((2, 128, T), np.float32)
    if not rope:
        tA[0] = 1.0
        tB[0, 64:96] = 1.0
        return tA, tB
    t = np.arange(T)
    row = (t // 64).astype(np.float32)
    col = (t % 64).astype(np.float32)
    for (tab, nf, base) in ((tA, 16, 0), (tB, 8, 64)):
        inv = (np.float32(10000.0) ** (-np.arange(nf, dtype=np.float32) / np.float32(nf))).astype(np.float32)
        ar = (row[:, None] * inv[None, :]).astype(np.float32)
        ac = (col[:, None] * inv[None, :]).astype(np.float32)
        cr, sr, cc, sc = np.cos(ar).T, np.sin(ar).T, np.cos(ac).T, np.sin(ac).T
        cosp = np.concatenate([cr, cr, cc, cc], 0).astype(np.float32)
        sinp = np.concatenate([-sr, sr, -sc, sc], 0).astype(np.float32)
        n = 4 * nf
        tab[0, base:base + n] = cosp
        tab[1, base:base + n] = sinp
        if base == 0:
            tab[0, 64:128] = cosp
            tab[1, 64:128] = sinp
    return tA, tB


CONST = {}
_coff = 0
for _n, _w in [("ident", 128), ("blk2", 128), ("ones", 128), ("CMf", 1024), ("CMb", 1024), ("NEGMf", 512),
               ("NEGMb", 512), ("I8", 512), ("Uf", 64), ("Ub", 64), ("nUf", 64), ("nUb", 64), ("noff", 64), ("MS", 384)]:
    CONST[_n] = (_coff, _w)
    _coff += _w
NCONST = _coff


def _consts():
    c = np.zeros((128, NCONST), np.float32)

    def put(n, a):
        o, w = CONST[n]
        a = np.asarray(a, np.float32).reshape(a.shape[0], -1)
        assert a.shape[1] == w
        c[:a.shape[0], o:o + w] = a
    put("ident", np.eye(128))
    b = np.zeros((128, 128)); b[:64, :64] = 1; b[64:, 64:] = 1
    put("blk2", b)
    put("ones", np.ones((128, 128)))
    k = np.arange(64)[:, None]; f = np.arange(64)[None, :]
    Uf = (k <= f).astype(np.float32); Ub = (k >= f).astype(np.float32)
    for nm, U in (("CMf", Uf), ("CMb", Ub)):
        cm = np.zeros((64, 2, 8, 64), np.float32)
        cm[:, 0] = U[:, None, :]
        cm[:, 1] = 1.0
        put(nm, cm)
    put("NEGMf", np.broadcast_to(np.where(k <= f, 0.0, -30000.0)[:, None, :], (64, 8, 64)).copy())
    put("NEGMb", np.broadcast_to(np.where(k >= f, 0.0, -30000.0)[:, None, :], (64, 8, 64)).copy())
    put("I8", np.broadcast_to(np.eye(64)[:, None, :], (64, 8, 64)).copy())
    put("Uf", Uf); put("Ub", Ub); put("nUf", -Uf); put("nUb", -Ub)
    put("noff", -(1.0 - np.eye(64)))
    ms = np.zeros((64, 6, 64), np.float32)
    pi = np.arange(64)[:, None]; fi = np.arange(64)[None, :]
    for lv in range(6):
        ms[:, lv, :] = ((pi >> (lv + 1)) == (fi >> (lv + 1))) & (((pi >> lv) & 1) != ((fi >> lv) & 1))
    put("MS", ms)
    return c


IN_SPECS = [
    ("xs", (TS, D)), ("xc", (TC, D)), ("cka", (L, PAST, 128)), ("cva", (L, PAST, 128)), ("cckv", (L, PAST, 256)),
    ("ckpe", (L, PAST, 32)), ("sf", (L, 8, 64, 64)), ("sb", (L, 8, 64, 64)), ("cv2", (2, D)),
    ("w_mod", (L, D, 6 * D)), ("b_mod", (L, 6 * D)), ("norm1", (L, D)), ("norm2", (L, D)), ("w_in_r", (L, D, NCOL)),
    ("aq", (L, 2, 64)), ("akn", (L, 2, 64)), ("b_qnorm", (L, 384)), ("b_kvnorm", (L, 256)),
    ("w_uq", (L, 384, 768)), ("w_uqs", (L, 384, 768)), ("w_ukv_k", (L, 256, 512)), ("w_ukv_v", (L, 256, 512)),
    ("c_conv", (L, 3, 1536)), ("c_alog", (L, 16)), ("c_dt", (L, 16)), ("c_onorm", (L, 64)),
    ("w_pa", (L, 512, D)), ("w_pb", (L, 512, D)), ("w_pc", (L, 512, D)), ("w_out", (L, D, D)),
    ("w_gate", (L, D, DFF)), ("w_up", (L, D, DFF)), ("w_down", (L, DFF, D)), ("final_norm", (D,)),
    ("tabA_S", (2, 128, TS)), ("tabB_S", (2, 128, TS)), ("tabA_C", (2, 128, TC)), ("tabB_C", (2, 128, TC)),
    ("consts", (128, NCONST)),
]
OUT_SPECS = [
    ("ys", (TS, D)), ("yc", (TC, D)), ("nka", (2, L, 256, 128)), ("nva", (2, L, 256, 128)),
    ("nckv", (2, L, 256, 256)), ("nkpe", (2, L, 256, 32)), ("nsf", (2, L, 8, 64, 64)), ("nsb", (2, L, 8, 64, 64)),
]


class PassCfg:
    def __init__(self, name):
        self.name = name
        if name == "S":
            self.T = TS; self.seqs = [(0, TS)]; self.ncache = PAST; self.row = 0
            self.blocks = [(0, i * 512, 512) for i in range(4)]
        else:
            self.T = TC; self.seqs = [(0, 256), (256, 256)]; self.ncache = 0; self.row = 32
            self.blocks = [(0, 0, 256), (1, 256, 256)]
        self.nch = self.T // 64


class _Stop(Exception):
    pass


class Builder:
    def stage(self, k):
        if self.cfg.get("stopat") == k and not self.kb.mute:
            self.kb.barrier()
            self.kb.finish()
            self.kb.mute = True

    def __init__(self, cfg=None):
        self.cfg = cfg or {}
        nc = self.nc = bass.Bass("TRN2", target_bir_lowering=False)
        self.I = {n: nc.dram_tensor(n, list(s), F32, kind="ExternalInput") for n, s in IN_SPECS}
        self.O = {n: nc.dram_tensor(n, list(s), F32, kind="ExternalOutput") for n, s in OUT_SPECS}
        self.modD = nc.dram_tensor("modD", [L, 33, 6 * D], F32, kind="Internal")
        self.xbuf = {"S": nc.dram_tensor("xbufS", [TS, D], F32, kind="Internal"),
                     "C": nc.dram_tensor("xbufC", [TC, D], F32, kind="Internal")}
        self.oD = {(pn, d): nc.dram_tensor("oD%s%d" % (pn, d), [T // 64, 64, 512], F32, kind="Internal")
                   for pn, T in (("S", TS), ("C", TC)) for d in (0, 1)}
        self.qkvD = {pn: nc.dram_tensor("qkvD%s" % pn, [128, 12, T], BF16, kind="Internal")
                     for pn, T in (("S", TS), ("C", TC))}
        self.wi = 0
        self.oi = 0
        self.pool_i = {}
        self.ring_i = {}

    def d(self, ap):
        return V(Tile("d", None), ap)

    def phase(self):
        b = self

        class _P:
            def __enter__(s):
                s.old = b.kb.st
                s.es = ExitStack()
                s.es.__enter__()
                b.kb.st = s.es
                return s

            def __exit__(s, *a):
                b.kb.barrier()
                b.kb.st = s.old
                return s.es.__exit__(*a)
        return _P()

    def pbank(self, pool):
        banks = {"g": [0, 1, 2, 3], "s": [0, 1, 2, 3], "t": [4, 5], "o": [6, 7],
                 "g0": [0, 1, 2], "g1": [3, 4, 5], "t0": [6], "t1": [7]}[pool]
        i = self.pool_i.get(pool, 0)
        self.pool_i[pool] = i + 1
        return self.PS[banks[i % len(banks)]]

    def ring(self, name, tiles):
        i = self.ring_i.get(name, 0)
        self.ring_i[name] = i + 1
        return tiles[i % len(tiles)]

    def bfv(self, ps):
        return V(ps, ps.t[:].bitcast(BF16))

    def alloc_wring(self):
        self.wring = [self.kb.sb("wr%d" % i, [128, 8, 512], BF16) for i in range(4)]
        self.wi = 0

    def wload(self, src_ap, K, n):
        t = self.wring[self.wi % len(self.wring)]
        self.wi += 1
        kc = K // 128
        self.kb.dma("pool", t[:, 0:kc, 0:n], self.d(src_ap.rearrange("(kc p) n -> p kc n", p=128)))
        return t

    def wstream(self, specs, ahead=2):
        tiles = []
        for i in range(min(ahead, len(specs))):
            tiles.append(self.wload(*specs[i]))
        for i in range(len(specs)):
            if i + ahead < len(specs):
                tiles.append(self.wload(*specs[i + ahead]))
            yield tiles[i]

    def win(self, l, grp, c0=0, n=None):
        o, w = GRP[grp]
        n = w - c0 if n is None else n
        return (self.I["w_in_r"][l, :, o + c0:o + c0 + n], D, n)

    def setup(self):
        kb = self.kb
        self.PS = [kb.ps("ps%d" % i, [128, 512], F32) for i in range(8)]
        cst = self.cst = kb.sb("cst", [128, 384], F32)
        kb.dma("sp", cst.ap(), self.d(self.I["consts"][:, 0:384]))
        self.ident_bf = kb.sb("ident_bf", [128, 128], BF16)
        self.blk2_bf = kb.sb("blk2_bf", [128, 128], BF16)
        self.ones_bf = kb.sb("ones_bf", [128, 128], BF16)
        for t, n in ((self.ident_bf, "ident"), (self.blk2_bf, "blk2"), (self.ones_bf, "ones")):
            o, w = CONST[n]
            kb.copy(t.ap(), cst[:, o:o + w])
        self.xt_ring = [kb.sb("xt%d" % i, [128, D], F32) for i in range(1)]
        self.xn_ring = [kb.sb("xn%d" % i, [128, D], BF16) for i in range(1)]
        self.junk_bf = kb.sb("junk", [128, D], BF16)
        self.small = [kb.sb("sm%d" % i, [128, 8], F32) for i in range(6)]

    def cv(self, n, rows=128):
        o, w = CONST[n]
        return self.cst[0:rows, o:o + w]

    def mods(self):
        kb = self.kb
        with self.phase():
            craw = kb.sb("craw", [128, 2, 8], F32)
            for r in range(2):
                kb.dma("sp", craw[:, r, :], self.d(self.I["cv2"][r, :].rearrange("(kc p) -> p kc", p=128)),
                       allow_slow_non_contiguous=True)
            sc33 = kb.sb("sc33", [128, 8, 33], F32)
            kb.memset(sc33.ap(), 0.0, eng="dve")
            kb.act(sc33[:, :, 0:1], craw[:, 0, :].re("p (k o) -> p k o", o=1), AF.Silu)
            kb.act(sc33[:, :, 32:33], craw[:, 1, :].re("p (k o) -> p k o", o=1), AF.Silu)
            wm = [kb.sb("wm%d" % i, [128, 8, 512], F32) for i in range(2)]
            mrow = [kb.sb("mrow%d" % i, [33, 512], F32) for i in range(2)]
            for l in range(L):
                for j in range(12):
                    w = wm[j % 2]
                    kb.dma("sp", w.ap(), self.d(self.I["w_mod"][l, :, j * 512:(j + 1) * 512].rearrange("(kc p) n -> p kc n", p=128)))
                    ps = self.pbank("g")
                    for kc in range(8):
                        kb.mm(ps[0:33, :], sc33[:, kc, :], w[:, kc, :], start=(kc == 0), stop=(kc == 7))
                    m = mrow[j % 2]
                    kb.copy(m.ap(), ps[0:33, :], eng="act" if j % 2 else "dve")
                    kb.dma("sp", V(self.modT, self.modD[l, :, j * 512:(j + 1) * 512]), m.ap())

    def layer_consts(self, p, l):
        kb = self.kb
        r = p.row
        mc = kb.sb("mc", [128, 48], F32)
        bc = kb.sb("bc", [128, 48], F32)
        kb.dma("sp", mc.ap(), V(self.modT, self.modD[l, r, :].rearrange("(m q) -> q m", q=128)), allow_slow_non_contiguous=True)
        kb.dma("sp", bc.ap(), self.d(self.I["b_mod"][l, :].rearrange("(m q) -> q m", q=128)), allow_slow_non_contiguous=True)
        kb.tt(mc.ap(), mc.ap(), bc.ap(), ALU.add)
        ncol = kb.sb("ncol", [128, 16], F32)
        kb.dma("sp", ncol[:, 0:8], self.d(self.I["norm1"][l, :].rearrange("(c q) -> q c", q=128)), allow_slow_non_contiguous=True)
        kb.dma("sp", ncol[:, 8:16], self.d(self.I["norm2"][l, :].rearrange("(c q) -> q c", q=128)), allow_slow_non_contiguous=True)
        AB = kb.sb("AB", [128, 32], F32)
        kb.stt(AB[:, 0:8], mc[:, 8:16], 1.0, ncol[:, 0:8], ALU.add, ALU.mult)
        kb.copy(AB[:, 8:16], mc[:, 0:8])
        kb.stt(AB[:, 16:24], mc[:, 32:40], 1.0, ncol[:, 8:16], ALU.add, ALU.mult)
        kb.copy(AB[:, 24:32], mc[:, 24:32])
        return AB

    def make_gb(self, p, l):
        kb = self.kb
        r = p.row
        gb = kb.sb("gb", [128, 2, D], F32)
        gtmp = self.xt_ring[0]
        for i, c0 in enumerate((2 * D, 5 * D)):
            kb.dma("sp", gb[:, i, :], V(self.modT, self.modD[l, r:r + 1, c0:c0 + D].to_broadcast([128, D])))
            kb.dma("sp", gtmp.ap(), self.d(self.I["b_mod"][l:l + 1, c0:c0 + D].to_broadcast([128, D])))
            kb.tt(gb[:, i, :], gb[:, i, :], gtmp.ap(), ALU.add)
        return gb

    def norm_one(self, xt, Acol, Bcol, hT, t0):
        kb = self.kb
        sm = self.ring("sm", self.small)
        kb.act(self.junk_bf.ap(), xt.ap(), AF.Square, accum_out=sm[:, 0:1])
        kb.act(sm[:, 1:2], sm[:, 0:1], AF.Sqrt, scale=1.0 / D, bias=EPS)
        kb.recip(sm[:, 2:3], sm[:, 1:2])
        xn = self.ring("xn", self.xn_ring)
        kb.act(xn.ap(), xt.ap(), AF.Identity, scale=sm[:, 2:3])
        pb = self.bfv(self.pbank("t"))
        for c in range(8):
            kb.tr(pb[:, c * 128:(c + 1) * 128], xn[:, c * 128:(c + 1) * 128], self.ident_bf.ap())
        for c in range(8):
            if c % 2 == 0:
                kb.ts(hT[:, c, t0:t0 + 128], pb[:, c * 128:(c + 1) * 128], Acol[:, c:c + 1], ALU.mult, Bcol[:, c:c + 1], ALU.add)
            else:
                kb.act(hT[:, c, t0:t0 + 128], pb[:, c * 128:(c + 1) * 128], AF.Identity, scale=Acol[:, c:c + 1], bias=Bcol[:, c:c + 1])
        return sm

    def xsrc(self, p, l, i):
        if l == 0:
            return self.d(self.I["xs" if p.name == "S" else "xc"][i * 128:(i + 1) * 128, :])
        return V(self.xT[p.name][i], self.xbuf[p.name][i * 128:(i + 1) * 128, :])

    def rms_rstd(self, px, n, lhs_ones, inv_n, first=True, last=True, pss=None):
        kb = self.kb
        sq = self.ring("sq", self.sq_ring)
        kb.act(sq[:, 0:n], px, AF.Square)
        if pss is None:
            pss = self.pbank("g")
        kb.mm(pss[:, 0:n], lhs_ones, sq[:, 0:n], start=first, stop=last)
        return pss

    def rstd_from(self, pss, n, inv_n):
        kb = self.kb
        rs = self.ring("rs", self.rs_ring)
        kb.act(rs[:, 0:n], pss[:, 0:n], AF.Sqrt, scale=inv_n, bias=EPS)
        kb.recip(rs[:, 0:n], rs[:, 0:n])
        return rs

    def rope_norm(self, px, pxs, n, lhs_ones, inv_n, gcols, cos, sin, out, rows=slice(0, 128)):
        kb = self.kb
        pss = self.rms_rstd(px[:, 0:n], n, lhs_ones, inv_n)
        rs = self.rstd_from(pss, n, inv_n)
        t1 = self.ring("t1", self.t1_ring)
        t2 = self.ring("t2", self.t2_ring)
        kb.tt(t1[:, 0:n], px[:, 0:n], rs[:, 0:n], ALU.mult)
        kb.stt(t1[:, 0:n], t1[:, 0:n], gcols[:, 0:1], cos, ALU.mult, ALU.mult)
        kb.tt(t2[:, 0:n], pxs[:, 0:n], rs[:, 0:n], ALU.mult)
        kb.stt(t2[:, 0:n], t2[:, 0:n], gcols[:, 1:2], sin, ALU.mult, ALU.mult)
        kb.tt(out, t1[:, 0:n], t2[:, 0:n], ALU.add, eng="pool")

    def pfm(self, ps, w, col0, M, actT, nkc, t0, n):
        for kc in range(nkc):
            self.kb.mm(ps, w[:, kc, col0:col0 + M], actT[:, kc, t0:t0 + n], start=(kc == 0), stop=(kc == nkc - 1))

    def attend(self, qv, kv, vv, nkt, Lq, scale, yT, tok0, hq):
        kb = self.kb
        r0 = (hq % 2) * 64
        c = hq // 2
        for q0 in range(0, Lq, 512):
            n = min(512, Lq - q0)
            nj = n // 128
            po = self.pbank("o")
            pTs = {}
            for kt in range(nkt + 1):
                if kt < nkt:
                    pst = self.pbank("s")
                    kb.mm(pst[:, 0:n], kv(kt), qv(q0, n))
                    pT = self.ring("pT", self.pT_ring)
                    kb.act(pT[:, 0:n], pst[:, 0:n], AF.Exp, scale=scale)
                    pTs[kt] = pT
                if kt >= 1:
                    k2 = kt - 1
                    pT = pTs.pop(k2)
                    for j in range(nj):
                        kb.mm(po[:, j * 65:(j + 1) * 65], pT[:, j * 128:(j + 1) * 128], vv(k2),
                              start=(k2 == 0 and j == 0), stop=(k2 == nkt - 1), skip_group_check=True)
            pb = self.bfv(self.pbank("t"))
            for j in range(nj):
                rc = self.ring("sm", self.small)
                kb.recip(rc[:, 0:1], po[:, j * 65 + 64:j * 65 + 65])
                yn = self.ring("yn", self.yn_ring)
                kb.ts(yn.ap(), po[:, j * 65:j * 65 + 64], rc[:, 0:1], ALU.mult)
                kb.tr(pb[0:64, j * 128:(j + 1) * 128], yn.ap(), self.ident_bf.ap())
            self.cp_i = getattr(self, "cp_i", 0) + 1
            kb.copy(yT[r0:r0 + 64, c, tok0 + q0:tok0 + q0 + n], pb[0:64, 0:n], eng="act" if self.cp_i % 2 else "dve")

    def alloc_attn_common(self, p):
        kb = self.kb
        self.sq_ring = [kb.sb("sq%d" % i, [128, 512], BF16) for i in range(2)]
        self.rs_ring = [kb.sb("rs%d" % i, [128, 512], F32) for i in range(1)]
        self.t1_ring = [kb.sb("t1%d" % i, [128, 512], F32) for i in range(1)]
        self.t2_ring = [kb.sb("t2%d" % i, [128, 512], F32) for i in range(1)]
        self.pT_ring = [kb.sb("pT%d" % i, [128, 512], BF16) for i in range(4)]
        self.cos_ring = [kb.sb("cos%d" % i, [128, 512], F32) for i in range(1)]
        self.sin_ring = [kb.sb("sin%d" % i, [128, 512], F32) for i in range(1)]
        self.yn_ring = [kb.sb("yn%d" % i, [128, 64], BF16) for i in range(4)]
        self.yT = kb.sb("yT", [128, 4, p.T], BF16)
        self.stg = [kb.sb("stg%d" % i, [128, 512], F32) for i in range(1)]

    def load_tabs(self, p, which, t0, n):
        kb = self.kb
        tab = self.I["tab%s_%s" % (which, p.name)]
        cos = self.ring("cos", self.cos_ring)
        sin = self.ring("sin", self.sin_ring)
        kb.dma("sp", cos[:, 0:n], self.d(tab[0, :, t0:t0 + n]))
        kb.dma("sp", sin[:, 0:n], self.d(tab[1, :, t0:t0 + n]))
        return cos, sin

    def mixerA(self, p, l, hT):
        kb = self.kb
        I = self.I
        T = p.T
        nseq = len(p.seqs)
        Ls = p.seqs[0][1]
        NK = p.ncache + Ls
        nkt = NK // 128
        self.alloc_attn_common(p)
        qT = kb.sb("qT", [128, 4, T], BF16)
        kT = [kb.sb("kT%d" % g, [128, nseq, NK], BF16) for g in range(2)]
        vA = kb.sb("vA", [128, nseq * nkt, 2, 65], BF16)
        kb.memset(vA[:, :, :, 64:65], 1.0, eng="dve")
        gq = kb.sb("gq", [128, 2], F32)
        gk = kb.sb("gk", [128, 2], F32)
        gkrow = kb.sb("gkrow", [128, 64], F32)
        for hf in range(2):
            kb.dma("sp", gq[hf * 64:(hf + 1) * 64, :], self.d(I["aq"][l].rearrange("r d -> d r")), allow_slow_non_contiguous=True)
            kb.dma("sp", gk[hf * 64:(hf + 1) * 64, :], self.d(I["akn"][l].rearrange("r d -> d r")), allow_slow_non_contiguous=True)
        kb.dma("sp", gkrow.ap(), self.d(I["akn"][l, 0:1, :].to_broadcast([128, 64])))
        if p.ncache:
            cbits = self.cfg.get("cbits", 7)
            for kt in range(p.ncache // 128):
                ck = self.ring("stg", self.stg)
                cd = self.ring("xn", self.xn_ring)
                if cbits & 1:
                    kb.dma("sp", ck[:, 0:128], self.d(I["cka"][l, kt * 128:(kt + 1) * 128, :]))
                    kb.copy(cd[:, 0:256].re("p (g r d) -> p g r d", g=2, r=2), ck[:, 0:128].re("p (g o d) -> p g o d", g=2, o=1).bc([128, 2, 2, 64]))
                if cbits & 2:
                    pb = self.bfv(self.pbank("t"))
                    for g in range(2):
                        kb.tr(pb[:, g * 128:(g + 1) * 128], cd[:, g * 128:(g + 1) * 128], self.ident_bf.ap())
                    for g in range(2):
                        kb.ts(kT[g][:, 0, kt * 128:(kt + 1) * 128], pb[:, g * 128:(g + 1) * 128], 1.0, ALU.mult)
                if cbits & 4:
                    kb.dma("sp", ck[:, 128:256], self.d(I["cva"][l, kt * 128:(kt + 1) * 128, :]))
                    kb.copy(vA[:, kt, :, 0:64], ck[:, 128:256].re("p (g d) -> p g d", g=2), eng="act")
        self.stage(3)
        wq, wqs, wk, wv = [self.wload(*s) for s in (self.win(l, "AQ"), self.win(l, "AQS"), self.win(l, "AK", 0, 512), self.win(l, "AV", 0, 256))]
        for (s, t0, n) in p.blocks:
            s0 = p.seqs[s][0]
            cos, sin = self.load_tabs(p, "A", t0, n)
            for c in range(4):
                px = self.pbank("g"); pxs = self.pbank("g")
                self.pfm(px[:, 0:n], wq, c * 128, 128, hT, 8, t0, n)
                self.pfm(pxs[:, 0:n], wqs, c * 128, 128, hT, 8, t0, n)
                self.rope_norm(px, pxs, n, self.blk2_bf.ap(), 1.0 / 64, gq, cos[:, 0:n], sin[:, 0:n], qT[:, c, t0:t0 + n])
            for g in range(2):
                px = self.pbank("g"); pxs = self.pbank("g")
                self.pfm(px[:, 0:n], wk, g * 128, 128, hT, 8, t0, n)
                self.pfm(pxs[:, 0:n], wk, 256 + g * 128, 128, hT, 8, t0, n)
                k0 = p.ncache + t0 - s0
                self.rope_norm(px, pxs, n, self.ones_bf.ap(), 1.0 / 128, gk, cos[:, 0:n], sin[:, 0:n], kT[g][:, s, k0:k0 + n])
            for i in range(n // 128):
                tt0 = t0 + i * 128
                pv = self.pbank("g")
                for kc in range(8):
                    kb.mm(pv[:, 0:256], hT[:, kc, tt0:tt0 + 128], wv[:, kc, 0:256], start=(kc == 0), stop=(kc == 7))
                kt = s * nkt + (p.ncache + tt0 - s0) // 128
                kb.copy(vA[:, kt, :, 0:64], pv[:, 0:128].re("p (g d) -> p g d", g=2), eng="act")
                if p.name == "C":
                    st = self.ring("stg", self.stg)
                    kb.copy(st[:, 0:128], pv[:, 0:128])
                    kb.dma("sp", self.d(self.O["nva"][s, l, tt0 - s0:tt0 - s0 + 128, :]), st[:, 0:128], is_output=True)
                    sm = self.ring("sm", self.small)
                    sqf = self.ring("t1", self.t1_ring)
                    kb.act(sqf[:, 0:128], pv[:, 128:256], AF.Square)
                    kb.reduce(sm[:, 0:2], sqf[:, 0:128].re("p (g d) -> p g d", g=2))
                    kb.act(sm[:, 2:4], sm[:, 0:2], AF.Sqrt, scale=1.0 / 64, bias=EPS)
                    kb.recip(sm[:, 2:4], sm[:, 2:4])
                    kb.tt(st[:, 128:256].re("p (g d) -> p g d", g=2), pv[:, 128:256].re("p (g d) -> p g d", g=2),
                          sm[:, 2:4].re("p (g o) -> p g o", o=1).bc([128, 2, 64]), ALU.mult)
                    kb.tt(st[:, 128:256].re("p (g d) -> p g d", g=2), st[:, 128:256].re("p (g d) -> p g d", g=2),
                          gkrow.ap().re("p (o d) -> p o d", o=1).bc([128, 2, 64]), ALU.mult)
                    kb.dma("sp", self.d(self.O["nka"][s, l, tt0 - s0:tt0 - s0 + 128, :]), st[:, 128:256], is_output=True)
        self.stage(4)
        for s, (s0, Lq) in enumerate(p.seqs):
            for hq in range(self.cfg.get("nheads", 8)):
                g = hq // 4
                r0 = (hq % 2) * 64
                c = hq // 2
                self.attend(lambda q0, n: qT[r0:r0 + 64, c, s0 + q0:s0 + q0 + n],
                            lambda kt: kT[g][r0:r0 + 64, s, kt * 128:(kt + 1) * 128],
                            lambda kt: vA[:, s * nkt + kt, g, :],
                            nkt, Lq, 0.125, self.yT, s0, hq)
        return self.yT

    def mixerB(self, p, l, hT):
        kb = self.kb
        I = self.I
        T = p.T
        nseq = len(p.seqs)
        Ls = p.seqs[0][1]
        NK = p.ncache + Ls
        nkt = NK // 128
        self.alloc_attn_common(p)
        cqn = kb.sb("cqn", [128, 3, T], BF16)
        ckvn = kb.sb("ckvn", [128, 2, nseq, NK], BF16)
        kB = kb.sb("kB", [128, nseq, NK], BF16)
        kpeT = kB
        vB = kb.sb("vB", [128, nseq * nkt, 8, 65], BF16)
        kb.memset(vB[:, :, :, 64:65], 1.0, eng="dve")
        gqc = kb.sb("gqc", [128, 3], F32)
        gkc = kb.sb("gkc", [128, 2], F32)
        gkrow = kb.sb("gkvrow", [128, 256], F32)
        kb.dma("sp", gqc.ap(), self.d(I["b_qnorm"][l, :].rearrange("(c q) -> q c", q=128)), allow_slow_non_contiguous=True)
        kb.dma("sp", gkc.ap(), self.d(I["b_kvnorm"][l, :].rearrange("(c q) -> q c", q=128)), allow_slow_non_contiguous=True)
        kb.dma("sp", gkrow.ap(), self.d(I["b_kvnorm"][l:l + 1, :].to_broadcast([128, 256])))
        if p.ncache:
            for kt in range(p.ncache // 128):
                ck = self.ring("stg", self.stg)
                kb.dma("sp", ck[:, 0:256], self.d(I["cckv"][l, kt * 128:(kt + 1) * 128, :]))
                kb.dma("sp", ck[:, 256:288], self.d(I["ckpe"][l, kt * 128:(kt + 1) * 128, :]))
                cd = self.ring("xn", self.xn_ring)
                kb.memset(cd[:, 256:384], 0.0, eng="dve")
                kb.copy(cd[:, 0:256], ck[:, 0:256])
                kb.copy(cd[:, 320:352], ck[:, 256:288])
                pb = self.bfv(self.pbank("t"))
                for c in range(3):
                    kb.tr(pb[:, c * 128:(c + 1) * 128], cd[:, c * 128:(c + 1) * 128], self.ident_bf.ap())
                kb.ts(ckvn[:, :, 0, kt * 128:(kt + 1) * 128], pb[:, 0:256].re("p (c t) -> p c t", c=2), 1.0, ALU.mult)
                kb.ts(kpeT[64:96, 0, kt * 128:(kt + 1) * 128], pb[64:96, 256:384], 1.0, ALU.mult)
        wcq = self.wload(*self.win(l, "BCQ"))
        wkv = self.wload(*self.win(l, "BCKV", 0, 256 + 96 + 96 + 32))
        for (s, t0, n) in p.blocks:
            s0 = p.seqs[s][0]
            k0 = p.ncache + t0 - s0
            cos, sin = self.load_tabs(p, "B", t0, n)
            pxs = [self.pbank("g") for _ in range(3)]
            pss = self.pbank("g")
            for c in range(3):
                self.pfm(pxs[c][:, 0:n], wcq, c * 128, 128, hT, 8, t0, n)
                self.rms_rstd(pxs[c][:, 0:n], n, self.ones_bf.ap(), 0, first=(c == 0), last=(c == 2), pss=pss)
            rs = self.rstd_from(pss, n, 1.0 / 384)
            for c in range(3):
                kb.stt(cqn[:, c, t0:t0 + n], pxs[c][:, 0:n], gqc[:, c:c + 1], rs[:, 0:n], ALU.mult, ALU.mult)
            pxs = [self.pbank("g") for _ in range(2)]
            pss = self.pbank("g")
            for c in range(2):
                self.pfm(pxs[c][:, 0:n], wkv, c * 128, 128, hT, 8, t0, n)
                self.rms_rstd(pxs[c][:, 0:n], n, self.ones_bf.ap(), 0, first=(c == 0), last=(c == 1), pss=pss)
            rs = self.rstd_from(pss, n, 1.0 / 256)
            for c in range(2):
                kb.stt(ckvn[:, c, s, k0:k0 + n], pxs[c][:, 0:n], gkc[:, c:c + 1], rs[:, 0:n], ALU.mult, ALU.mult)
            p1 = self.pbank("g"); p2 = self.pbank("g")
            self.pfm(p1[0:96, 0:n], wkv, 256, 96, hT, 8, t0, n)
            self.pfm(p2[0:96, 0:n], wkv, 352, 96, hT, 8, t0, n)
            t1 = self.ring("t1", self.t1_ring); t2 = self.ring("t2", self.t2_ring)
            kb.tt(t1[64:96, 0:n], p1[64:96, 0:n], cos[64:96, 0:n], ALU.mult)
            kb.tt(t2[64:96, 0:n], p2[64:96, 0:n], sin[64:96, 0:n], ALU.mult)
            kb.tt(kpeT[64:96, s, k0:k0 + n], t1[64:96, 0:n], t2[64:96, 0:n], ALU.add, eng="pool")
            if p.name == "C":
                for i in range(n // 128):
                    tt0 = t0 + i * 128
                    pv = self.pbank("g")
                    for kc in range(8):
                        kb.mm(pv[:, 0:288], hT[:, kc, tt0:tt0 + 128], wkv[:, kc, 0:288], start=(kc == 0), stop=(kc == 7))
                    for kc in range(8):
                        kb.mm(pv[:, 288:320], hT[:, kc, tt0:tt0 + 128], wkv[:, kc, 448:480], start=False, stop=(kc == 7), skip_group_check=True)
                    st = self.ring("stg", self.stg)
                    sm = self.ring("sm", self.small)
                    kb.act(st[:, 0:256], pv[:, 0:256], AF.Square, accum_out=sm[:, 0:1])
                    kb.act(sm[:, 1:2], sm[:, 0:1], AF.Sqrt, scale=1.0 / 256, bias=EPS)
                    kb.recip(sm[:, 2:3], sm[:, 1:2])
                    kb.stt(st[:, 0:256], pv[:, 0:256], sm[:, 2:3], gkrow.ap(), ALU.mult, ALU.mult)
                    kb.dma("sp", self.d(self.O["nckv"][s, l, tt0 - s0:tt0 - s0 + 128, :]), st[:, 0:256], is_output=True)
                    kb.copy(st[:, 256:288], pv[:, 288:320], eng="act")
                    kb.dma("sp", self.d(self.O["nkpe"][s, l, tt0 - s0:tt0 - s0 + 128, :]), st[:, 256:288], is_output=True)
        wv = self.wload(I["w_ukv_v"][l, :, :], 256, 512)
        wk = self.wload(I["w_ukv_k"][l, :, :], 256, 512)
        wq = [self.wload(I["w_uq"][l, :, i * 384:(i + 1) * 384], 384, 384) for i in range(2)]
        for s in range(nseq):
            for kt in range(nkt):
                pv = self.pbank("g")
                for kc in range(2):
                    kb.mm(pv.ap(), ckvn[:, kc, s, kt * 128:(kt + 1) * 128], wv[:, kc, :], start=(kc == 0), stop=(kc == 1))
                kb.copy(vB[:, s * nkt + kt, :, 0:64], pv.ap().re("p (h d) -> p h d", h=8), eng="act" if kt % 2 else "dve")
        qBs = [kb.sb("qB%d" % i, [128, T], BF16) for i in range(1)]
        wqs_t = [None, None]
        for h in range(8):
            qB = qBs[0]
            if h == 0:
                wqs_t = [kb.sb("wqs", [128, 3, 768], BF16)]
                for i in range(2):
                    kb.dma("pool", wqs_t[0][:, :, i * 384:(i + 1) * 384], self.d(I["w_uqs"][l, :, i * 384:(i + 1) * 384].rearrange("(kc q) n -> q kc n", q=128)))
            for s in range(nseq):
                for b0 in range(0, NK, 512):
                    n = min(512, NK - b0)
                    pk = self.pbank("g")
                    for kc in range(2):
                        kb.mm(pk[0:64, 0:n], wk[:, kc, h * 64:(h + 1) * 64], ckvn[:, kc, s, b0:b0 + n], start=(kc == 0), stop=(kc == 1))
                    kb.copy(kB[0:64, s, b0:b0 + n], pk[0:64, 0:n], eng="act")
            for (s, t0, n) in p.blocks:
                cos, sin = self.load_tabs(p, "B", t0, n)
                p1 = self.pbank("g"); p2 = self.pbank("g")
                w1 = wq[h // 4]; c1 = (h % 4) * 96
                self.pfm(p1[0:96, 0:n], w1, c1, 96, cqn, 3, t0, n)
                self.pfm(p2[0:96, 0:n], wqs_t[0], h * 96, 96, cqn, 3, t0, n)
                kb.copy(qB[0:64, t0:t0 + n], p1[0:64, 0:n], eng="act")
                t1 = self.ring("t1", self.t1_ring); t2 = self.ring("t2", self.t2_ring)
                kb.tt(t1[64:96, 0:n], p1[64:96, 0:n], cos[64:96, 0:n], ALU.mult)
                kb.tt(t2[64:96, 0:n], p2[64:96, 0:n], sin[64:96, 0:n], ALU.mult)
                kb.tt(qB[64:96, t0:t0 + n], t1[64:96, 0:n], t2[64:96, 0:n], ALU.add, eng="pool")
            for s, (s0, Lq) in enumerate(p.seqs):
                self.attend(lambda q0, n: qB[0:96, s0 + q0:s0 + q0 + n],
                            lambda kt: kB[0:96, s, kt * 128:(kt + 1) * 128],
                            lambda kt: vB[:, s * nkt + kt, h, :],
                            nkt, Lq, 96 ** -0.5, self.yT, s0, h)
        return self.yT

    def mixerC(self, p, l, hT):
        kb = self.kb
        I = self.I
        T = p.T
        nch = p.nch
        pn = p.name
        gall = kb.sb("gall", [64, nch, 16], F32)
        ball = kb.sb("ball", [64, nch, 16], F32)
        qkvD = self.qkvD[pn]
        qtiles = self.qkvTile[pn]
        with self.phase():
            self.alloc_wring()
            cw = kb.sb("cw", [128, 12, 3], F32)
            for k in range(3):
                kb.dma("sp", cw[:, :, k], self.d(I["c_conv"][l, k, :].rearrange("(c q) -> q c", q=128)), allow_slow_non_contiguous=True)
            stage = kb.sb("stage", [128, 12, 512], BF16)
            raw = kb.sb("raw", [128, 514], F32)
            acc = kb.sb("acc", [128, 512], F32)
            av = kb.sb("av", [128, 512], F32)
            sqb = kb.sb("sqb", [128, 512], BF16)
            rs = kb.sb("rsc", [128, 512], F32)
            wt = [self.wload(*self.win(l, "CQKV", i * 512, 512)) for i in range(3)]
            for bi, (s, t0, n) in enumerate(p.blocks):
                s0, Ls = p.seqs[s]
                has_l = t0 > s0
                has_r = t0 + n < s0 + Ls
                for c in range(12):
                    w = wt[c // 4]
                    col = (c % 4) * 128
                    pr = self.pbank("g")
                    self.pfm(pr[:, 0:n], w, col, 128, hT, 8, t0, n)
                    if has_l or has_r:
                        ph = self.pbank("g")
                    if has_l:
                        self.pfm(ph[:, 0:1], w, col, 128, hT, 8, t0 - 1, 1)
                    if has_r:
                        self.pfm(ph[:, 2:3], w, col, 128, hT, 8, t0 + n, 1)
                    kb.copy(raw[:, 1:n + 1], pr[:, 0:n], eng="act")
                    if has_l:
                        kb.copy(raw[:, 0:1], ph[:, 0:1])
                    else:
                        kb.memset(raw[:, 0:1], 0.0, eng="dve")
                    if has_r:
                        kb.copy(raw[:, n + 1:n + 2], ph[:, 2:3])
                    else:
                        kb.memset(raw[:, n + 1:n + 2], 0.0, eng="dve")
                    kb.ts(acc[:, 0:n], raw[:, 1:n + 1], cw[:, c, 1:2], ALU.mult)
                    kb.stt(acc[:, 0:n], raw[:, 0:n], cw[:, c, 0:1], acc[:, 0:n], ALU.mult, ALU.add)
                    kb.stt(acc[:, 0:n], raw[:, 2:n + 2], cw[:, c, 2:3], acc[:, 0:n], ALU.mult, ALU.add)
                    if c < 8:
                        kb.act(av[:, 0:n], acc[:, 0:n], AF.Silu)
                        kb.act(sqb[:, 0:n], av[:, 0:n], AF.Square)
                        pss = self.pbank("g")
                        kb.mm(pss[:, 0:n], self.blk2_bf.ap(), sqb[:, 0:n])
                        kb.act(rs[:, 0:n], pss[:, 0:n], AF.Sqrt, bias=EPS)
                        kb.recip(rs[:, 0:n], rs[:, 0:n])
                        if c < 4:
                            kb.stt(stage[:, c, 0:n], av[:, 0:n], 0.125, rs[:, 0:n], ALU.mult, ALU.mult)
                        else:
                            kb.tt(stage[:, c, 0:n], av[:, 0:n], rs[:, 0:n], ALU.mult)
                    else:
                        kb.act(stage[:, c, 0:n], acc[:, 0:n], AF.Silu)
                kb.dma("sp", V(qtiles[bi], qkvD[:, :, t0:t0 + n]), stage[:, :, 0:n])
            wab = self.wload(*self.win(l, "CAB"))
            abraw = kb.sb("abraw", [64, nch, 32], F32)
            for ch0 in range(0, nch, 16):
                m = min(16, nch - ch0)
                pa = self.pbank("g")
                for j in range(m):
                    ch = ch0 + j
                    for kc in range(8):
                        kb.mm(pa[0:64, j * 32:(j + 1) * 32], hT[:, kc, ch * 64:(ch + 1) * 64], wab[:, kc, 0:32],
                              start=(kc == 0), stop=(kc == 7), skip_group_check=True)
                kb.copy(abraw[:, ch0:ch0 + m, :], pa[0:64, 0:m * 32].re("p (c k) -> p c k", k=32))
            dtb = kb.sb("dtb", [64, 16], F32)
            negA = kb.sb("negA", [64, 16], F32)
            kb.dma("sp", dtb.ap(), self.d(I["c_dt"][l:l + 1, :].to_broadcast([64, 16])))
            kb.dma("sp", negA.ap(), self.d(I["c_alog"][l:l + 1, :].to_broadcast([64, 16])))
            kb.act(negA.ap(), negA.ap(), AF.Exp)
            kb.ts(negA.ap(), negA.ap(), -1.0, ALU.mult)
            xa = kb.sb("xa", [64, nch, 16], F32)
            kb.tt(xa.ap(), abraw[:, :, 0:16], dtb.ap().re("p (o k) -> p o k", o=1).bc([64, nch, 16]), ALU.add)
            kb.act(xa.ap(), xa.ap(), AF.Exp)
            kb.act(xa.ap(), xa.ap(), AF.Ln, bias=1.0)
            kb.tt(gall.ap(), xa.ap(), negA.ap().re("p (o k) -> p o k", o=1).bc([64, nch, 16]), ALU.mult)
            kb.act(xa.ap(), abraw[:, :, 16:32], AF.Exp, scale=-1.0)
            kb.ts(xa.ap(), xa.ap(), 1.0, ALU.add)
            kb.recip(ball.ap(), xa.ap())
        with self.phase():
            self.scan(p, l, gall, ball)
        if self.cfg.get("cphase", 3) < 3:
            return yT
        yT = kb.sb("yTc", [128, 4, T], BF16)
        with self.phase():
            self.alloc_wring()
            wz = self.wload(*self.win(l, "CZ"))
            gon = kb.sb("gon", [128, 1], F32)
            for hf in range(2):
                kb.dma("sp", gon[hf * 64:(hf + 1) * 64, :], self.d(I["c_onorm"][l, :].rearrange("(d o) -> d o", o=1)), allow_slow_non_contiguous=True)
            of_r = [kb.sb("of%d" % i, [128, 512], F32) for i in range(2)]
            ob_r = [kb.sb("ob%d" % i, [128, 512], F32) for i in range(2)]
            os_r = [kb.sb("os%d" % i, [128, 512], F32) for i in range(2)]
            sq_r = [kb.sb("osq%d" % i, [128, 512], F32) for i in range(1)]
            on_r = [kb.sb("on%d" % i, [128, 512], BF16) for i in range(2)]
            sz_r = [kb.sb("sz%d" % i, [128, 128], F32) for i in range(2)]
            for i in range(T // 128):
                of = self.ring("of", of_r); ob = self.ring("ob", ob_r)
                for hf in range(2):
                    ch = 2 * i + hf
                    kb.dma("sp", of[hf * 64:(hf + 1) * 64, :], V(self.oTile[(pn, 0)][ch], self.oD[(pn, 0)][ch]))
                    kb.dma("sp", ob[hf * 64:(hf + 1) * 64, :], V(self.oTile[(pn, 1)][ch], self.oD[(pn, 1)][ch]))
                osum = self.ring("os", os_r)
                kb.tt(osum.ap(), of.ap(), ob.ap(), ALU.add, eng="pool")
                sq = self.ring("osq", sq_r)
                kb.act(sq.ap(), osum.ap(), AF.Square)
                sm = self.ring("sm", self.small)
                kb.reduce(sm[:, 0:8], sq.ap().re("p (h f) -> p h f", h=8))
                sm2 = self.ring("sm", self.small)
                kb.act(sm2[:, 0:8], sm[:, 0:8], AF.Sqrt, scale=1.0 / 64, bias=EPS)
                kb.recip(sm2[:, 0:8], sm2[:, 0:8])
                on = self.ring("on", on_r)
                kb.tt(on.ap().re("p (h f) -> p h f", h=8), osum.ap().re("p (h f) -> p h f", h=8),
                      sm2[:, 0:8].re("p (h f) -> p h f", f=1).bc([128, 8, 64]), ALU.mult)
                pb = self.bfv(self.pbank("t"))
                for c in range(4):
                    kb.tr(pb[:, c * 128:(c + 1) * 128], on[:, c * 128:(c + 1) * 128], self.ident_bf.ap())
                for c in range(4):
                    pz = self.pbank("g")
                    self.pfm(pz[:, 0:128], wz, c * 128, 128, hT, 8, i * 128, 128)
                    sz = self.ring("sz", sz_r)
                    kb.act(sz.ap(), pz[:, 0:128], AF.Silu)
                    kb.stt(yT[:, c, i * 128:(i + 1) * 128], pb[:, c * 128:(c + 1) * 128], gon[:, 0:1], sz.ap(), ALU.mult, ALU.mult)
        return yT

    def scan(self, p, l, gall, ball):
        kb = self.kb
        I = self.I
        pn = p.name
        qkvD = self.qkvD[pn]
        qtiles = self.qkvTile[pn]
        blen = p.blocks[0][2]
        r32 = lambda v: v.bitcast(F32R)
        h3 = lambda v: v.re("p (h f) -> p h f", h=8)
        hs = lambda h: slice(h * 64, (h + 1) * 64)
        o_i8 = CONST["I8"][0]
        dk = kb.sb("dk", [64, NCONST - o_i8], F32)
        kb.dma("sp", dk.ap(), self.d(I["consts"][0:64, o_i8:NCONST]))

        def cd(n):
            o, w = CONST[n]
            return dk[:, o - o_i8:o - o_i8 + w]
        ones64 = self.cv("ones", 64)[:, 0:64]
        identf = self.cv("ident", 64)[:, 0:64]
        cR = kb.sb("cR", [64, 4, 64], F32)
        for i_, src_ in enumerate((ones64, cd("nUf"), cd("nUb"), cd("noff"))):
            kb.copy(r32(cR[:, i_, :]), src_)
        onesR = r32(cR[:, 0, :])
        noff = r32(cR[:, 3, :])
        I8 = cd("I8")
        MSv = lambda lv: cd("MS")[:, lv * 64:(lv + 1) * 64].re("p (o f) -> p o f", o=1).bc([64, 8, 64])
        Sstage = kb.sb("Sstage", [64, 8, 64], F32)
        mask_eng = self.cfg.get("mask_eng", "pool")

        def stream(d):
            u = lambda nm, w=512: kb.sb("s%d%s" % (d, nm), [64, w], F32)
            R = u("R", 1024); E = u("E"); Rb = u("Rb"); tmpA = u("tmpA"); Q = u("Q"); QT = u("QT")
            Dr = [u("D0"), u("D1")]; DTr = [u("DT0"), u("DT1")]
            Nq = u("Nq"); NqT = u("NqT"); Y = u("Y"); Y2 = u("Y2"); QKT = u("QKT")
            S = kb.sb("s%dS" % d, [64, 8, 64], F32)
            dkd = kb.sb("s%ddkd" % d, [64, 1536], F32)
            dsm = [kb.sb("s%ddsm%d" % (d, i), [64, 48], F32) for i in range(3)]
            qk_ring = [kb.sb("s%dqk%d" % (d, i), [128, 12, 64], BF16) for i in range(2)]
            qk0_ring = [kb.sb("s%dqk0_%d" % (d, i), [64, 16, 64], BF16) for i in range(2)]
            qf = kb.sb("s%dqf" % d, [64, 8, 64], F32)
            gp = "g%d" % d
            tp = "t%d" % d
            oc, _ = CONST["CMf" if d == 0 else "CMb"]
            on_, _ = CONST["NEGMf" if d == 0 else "NEGMb"]
            kb.dma("sp", dkd[:, 0:1024], self.d(I["consts"][0:64, oc:oc + 1024]))
            kb.dma("sp", dkd[:, 1024:1536], self.d(I["consts"][0:64, on_:on_ + 512]))
            CMd = dkd[:, 0:1024]
            NEGMd = dkd[:, 1024:1536]
            U = cd("Uf" if d == 0 else "Ub")
            nU = r32(cR[:, 1 + d, :])
            rq = [0]

            def load_qk(ch):
                t = qk_ring[rq[0] % 2]
                t0_ = qk0_ring[rq[0] % 2]
                rq[0] += 1
                qt = qtiles[(ch * 64) // blen]
                kb.dma("sp", t.ap(), V(qt, qkvD[:, :, ch * 64:(ch + 1) * 64]))
                for hf in range(2):
                    kb.dma("sp", t0_.ap().re("p (c two) t -> p c two t", two=2)[:, :, hf, :],
                           V(qt, qkvD[hf * 64:(hf + 1) * 64, 0:8, ch * 64:(ch + 1) * 64]))
                return t, t0_
            dsi = [0]
            for s, (s0, Ls) in enumerate(p.seqs):
                if p.ncache:
                    kb.dma("sp", Sstage.ap(), self.d(I["sf" if d == 0 else "sb"][l].rearrange("h d v -> d h v")))
                else:
                    kb.memset(Sstage.ap(), 0.0, eng="dve")
                kb.copy(r32(S.ap()), Sstage.ap())
                chs = list(range(s0 // 64, (s0 + Ls) // 64))
                if d == 1:
                    chs = chs[::-1]
                qk_next = load_qk(chs[0])
                for ci, ch in enumerate(chs):
                    qk, qk0 = qk_next
                    if ci + 1 < len(chs):
                        qk_next = load_qk(chs[ci + 1])
                    g = gall[:, ch, d * 8:(d + 1) * 8]
                    b = ball[:, ch, d * 8:(d + 1) * 8]
                    Kt = lambda h: qk0[:, 8 + h, :]
                    Qt = lambda h: qk0[:, h, :]
                    pg = self.pbank(gp)
                    kb.mm(pg[0:64, 0:8], U, g, skip_group_check=True)
                    kb.mm(pg[0:64, 8:16], ones64, g, skip_group_check=True)
                    sm = dsm[dsi[0] % 3]
                    dsi[0] += 1
                    gcs = sm[:, 0:8]; egc = sm[:, 8:16]; egl = sm[:, 16:24]; ek = sm[:, 24:32]; bg = sm[:, 32:40]
                    kb.copy(gcs, pg[0:64, 0:8])
                    kb.act(egc, gcs, AF.Exp)
                    kb.act(egl, pg[0:64, 8:16], AF.Exp)
                    kb.tt(ek, pg[0:64, 8:16], gcs, ALU.subtract)
                    kb.act(ek, ek, AF.Exp)
                    kb.tt(bg, b, egc, ALU.mult)
                    yield
                    kb.tt(r32(R.ap()).re("p (o h f) -> p o h f", o=2, h=8), CMd.re("p (o h f) -> p o h f", o=2, h=8),
                          g.re("p (o h f) -> p o h f", o=1, f=1).bc([64, 2, 8, 64]), ALU.mult)
                    pdf = self.pbank(gp)
                    kb.mm(pdf[0:64, :], onesR, r32(R[:, 0:512]), start=True, stop=False)
                    kb.mm(pdf[0:64, :], nU, r32(R[:, 512:1024]), start=False, stop=True)
                    kb.stt(E.ap(), pdf[0:64, :], 0.0, NEGMd, ALU.min, ALU.add)
                    kb.act(E.ap(), E.ap(), AF.Exp)
                    yield
                    kb.tt(h3(r32(Rb.ap())), h3(I8), b.re("p (h f) -> p h f", f=1).bc([64, 8, 64]), ALU.mult)
                    pbr = self.pbank(gp)
                    kb.mm(pbr[0:64, :], noff, r32(Rb.ap()))
                    pG = self.pbank(gp)
                    for h in range(8):
                        kb.mm(pG[0:64, hs(h)], Kt(h), Kt(h), skip_group_check=True)
                    yield
                    kb.tt(tmpA.ap(), pG[0:64, :], E.ap(), ALU.mult)
                    kb.tt(Q.ap(), tmpA.ap(), pbr[0:64, :], ALU.mult)
                    pKQ = self.pbank(gp)
                    for h in range(8):
                        kb.mm(pKQ[0:64, hs(h)], Kt(h), Qt(h), skip_group_check=True)
                    kb.tt(r32(QKT.ap()), pKQ[0:64, :], E.ap(), ALU.mult)
                    yield
                    pT = self.pbank(gp)
                    for h in range(8):
                        kb.tr(pT[0:64, hs(h)], Q[:, hs(h)], identf)
                    kb.copy(QT.ap(), pT[0:64, :], eng="act")
                    yield
                    D = Dr[0]; DT = DTr[0]
                    kb.tt(h3(tmpA.ap()), h3(Q.ap()), MSv(0), ALU.mult, eng=mask_eng)
                    kb.tt(r32(D.ap()), tmpA.ap(), I8, ALU.add)
                    kb.tt(h3(E.ap()), h3(QT.ap()), MSv(0), ALU.mult, eng=mask_eng)
                    kb.tt(r32(DT.ap()), E.ap(), I8, ALU.add)
                    yield
                    for lv in range(1, 6):
                        kb.tt(h3(r32(NqT.ap())), h3(QT.ap()), MSv(lv), ALU.mult, eng=mask_eng)
                        if lv < 5:
                            kb.tt(h3(r32(Nq.ap())), h3(Q.ap()), MSv(lv), ALU.mult, eng=mask_eng)
                        pY = self.pbank(gp)
                        for h in range(8):
                            kb.mm(pY[0:64, hs(h)], r32(NqT[:, hs(h)]), r32(D[:, hs(h)]), skip_group_check=True)
                        kb.tt(r32(Y.ap()), pY[0:64, :], I8, ALU.add)
                        if lv < 5:
                            pY2 = self.pbank(gp)
                            for h in range(8):
                                kb.mm(pY2[0:64, hs(h)], r32(Nq[:, hs(h)]), r32(DT[:, hs(h)]), skip_group_check=True)
                            kb.tt(r32(Y2.ap()), pY2[0:64, :], I8, ALU.add)
                        yield
                        pZ = self.pbank(gp)
                        for h in range(8):
                            kb.mm(pZ[0:64, hs(h)], r32(DT[:, hs(h)]), r32(Y[:, hs(h)]), skip_group_check=True)
                        Dn = Dr[lv % 2]
                        kb.copy(r32(Dn.ap()), pZ[0:64, :], eng="act")
                        if lv < 5:
                            pZ2 = self.pbank(gp)
                            for h in range(8):
                                kb.mm(pZ2[0:64, hs(h)], r32(D[:, hs(h)]), r32(Y2[:, hs(h)]), skip_group_check=True)
                            DTn = DTr[lv % 2]
                            kb.copy(r32(DTn.ap()), pZ2[0:64, :], eng="act")
                            DT = DTn
                        D = Dn
                        yield
                    P = D
                    Ktok = Q; Vtok = QT; Vb = Nq; Kbg = NqT; Kt2 = Y; wT = Y2
                    vnew = DTr[0]; o = DTr[1]; u_ = E; o1 = tmpA
                    pb = self.bfv(self.pbank(tp))
                    for c in range(4):
                        kb.tr(pb[0:64, c * 128:(c + 1) * 128], qk[:, 4 + c, :], self.ident_bf.ap())
                    kb.copy(Ktok.ap(), pb[0:64, 0:512], eng="act")
                    kb.tt(h3(r32(Kbg.ap())), h3(Ktok.ap()), bg.re("p (h f) -> p h f", f=1).bc([64, 8, 64]), ALU.mult)
                    kb.tt(h3(r32(Kt2.ap())), h3(Ktok.ap()), ek.re("p (h f) -> p h f", f=1).bc([64, 8, 64]), ALU.mult)
                    yield
                    pb2 = self.bfv(self.pbank(tp))
                    for c in range(4):
                        kb.tr(pb2[0:64, c * 128:(c + 1) * 128], qk[:, 8 + c, :], self.ident_bf.ap())
                    kb.copy(Vtok.ap(), pb2[0:64, 0:512], eng="act")
                    kb.tt(h3(r32(Vb.ap())), h3(Vtok.ap()), b.re("p (h f) -> p h f", f=1).bc([64, 8, 64]), ALU.mult)
                    yield
                    pu = self.pbank(gp)
                    for h in range(8):
                        kb.mm(pu[0:64, hs(h)], r32(P[:, hs(h)]), r32(Vb[:, hs(h)]), skip_group_check=True)
                    kb.copy(u_.ap(), pu[0:64, :], eng="act")
                    pw = self.pbank(gp)
                    for h in range(8):
                        kb.mm(pw[0:64, hs(h)], r32(Kbg[:, hs(h)]), r32(P[:, hs(h)]), skip_group_check=True)
                    kb.copy(r32(wT.ap()), pw[0:64, :])
                    kb.copy(r32(qf.ap()), qk0[:, 0:8, :], eng="act")
                    yield
                    pws = self.pbank(gp)
                    for h in range(8):
                        kb.mm(pws[0:64, hs(h)], r32(wT[:, hs(h)]), r32(S[:, h, :]), skip_group_check=True)
                    kb.tt(r32(vnew.ap()), u_.ap(), pws[0:64, :], ALU.subtract)
                    poa = self.pbank(gp)
                    for h in range(8):
                        kb.mm(poa[0:64, hs(h)], r32(qf[:, h, :]), r32(S[:, h, :]), skip_group_check=True)
                    kb.tt(h3(o1.ap()), h3(poa[0:64, :]), egc.re("p (h f) -> p h f", f=1).bc([64, 8, 64]), ALU.mult)
                    yield
                    pob = self.pbank(gp)
                    for h in range(8):
                        kb.mm(pob[0:64, hs(h)], r32(QKT[:, hs(h)]), r32(vnew[:, hs(h)]), skip_group_check=True)
                    kb.tt(r32(o.ap()), o1.ap(), pob[0:64, :], ALU.add)
                    kb.dma("sp", V(self.oTile[(pn, d)][ch], self.oD[(pn, d)][ch]), o.ap())
                    pS = self.pbank(gp)
                    for h in range(8):
                        kb.mm(pS[0:64, hs(h)], r32(Kt2[:, hs(h)]), r32(vnew[:, hs(h)]), skip_group_check=True)
                    kb.tt(r32(S.ap()), S.ap(), egl.re("p (h f) -> p h f", f=1).bc([64, 8, 64]), ALU.mult)
                    kb.tt(r32(S.ap()), S.ap(), pS[0:64, :].re("p (h f) -> p h f", h=8), ALU.add)
                    yield
                if pn == "C":
                    kb.dma("sp", self.d(self.O["nsf" if d == 0 else "nsb"][s, l].rearrange("h d v -> d h v")), S.ap(), is_output=True)
        ns = self.cfg.get("nstreams", 2)
        if ns == 2:
            gens = [stream(0), stream(1)]
        else:
            def seq():
                yield from stream(0)
                yield from stream(1)
            gens = [seq()]
        while gens:
            for g_ in list(gens):
                try:
                    next(g_)
                except StopIteration:
                    gens.remove(g_)

    def merge(self, p, l, hT, yT, merged, wname, xi):
        kb = self.kb
        sg_ring = [kb.sb("sg%d" % i, [128, 512], F32) for i in range(1)]
        pr_ring = [kb.sb("pr%d" % i, [128, 512], BF16) for i in range(1)]
        wts = [(self.wload(self.I[wname][l, :, hh * 512:(hh + 1) * 512], 512, 512),
                self.wload(*self.win(l, "G", xi * D + hh * 512, 512))) for hh in range(2)]
        for half in range(2):
            wp, wg = wts[half]
            for (s, t0, n) in p.blocks:
                for c in range(4):
                    pa = self.pbank("g"); pg = self.pbank("g")
                    self.pfm(pa[:, 0:n], wp, c * 128, 128, yT, 4, t0, n)
                    self.pfm(pg[:, 0:n], wg, c * 128, 128, hT, 8, t0, n)
                    sg = self.ring("sg", sg_ring)
                    kb.act(sg[:, 0:n], pg[:, 0:n], AF.Sigmoid)
                    dst = merged[:, half * 4 + c, t0:t0 + n]
                    if xi == self.first_mixer:
                        kb.tt(dst, pa[:, 0:n], sg[:, 0:n], ALU.mult)
                    else:
                        pr = self.ring("pr", pr_ring)
                        kb.tt(pr[:, 0:n], pa[:, 0:n], sg[:, 0:n], ALU.mult)
                        kb.tt(dst, dst, pr[:, 0:n], ALU.add, eng="pool")

    def tail(self, p, l, merged, AB):
        kb = self.kb
        I = self.I
        last = (l == L - 1)
        gb = self.make_gb(p, l)
        if last:
            self.fnb = kb.sb("fnb", [128, D], F32)
            kb.dma("sp", self.fnb.ap(), self.d(self.I["final_norm"][None, :].to_broadcast([128, D])))
        wout = [kb.sb("wout%d" % i, [128, 8, 512], BF16) for i in range(2)]
        for h in range(2):
            kb.dma("pool", wout[h].ap(), self.d(I["w_out"][l, :, h * 512:(h + 1) * 512].rearrange("(kc q) n -> q kc n", q=128)))
        h2T = kb.sb("h2T", [128, 8, 512], BF16)
        actT = kb.sb("actT", [128, 22, 512], BF16)
        wd = kb.sb("wd", [128, 22, 512], BF16)
        xm = [kb.sb("xm%d" % i, [128, D], F32) for i in range(4)]
        tmp_ring = [kb.sb("tmp%d" % i, [128, 512], F32) for i in range(2)]
        sgb_ring = [kb.sb("sgb%d" % i, [128, 512], BF16) for i in range(2)]
        yo_ring = [kb.sb("yo%d" % i, [128, D], F32) for i in range(2)]
        fspecs = []
        for f0 in range(0, DFF, 512):
            fn = min(512, DFF - f0)
            fspecs.append((I["w_gate"][l, :, f0:f0 + fn], D, fn))
            fspecs.append((I["w_up"][l, :, f0:f0 + fn], D, fn))
        for b0 in range(0, p.T, 512):
            it = self.wstream(fspecs)
            pre = [next(it), next(it)]
            for i in range(4):
                ti = b0 // 128 + i
                xt = self.ring("xt", self.xt_ring)
                kb.dma("sp", xt.ap(), self.xsrc(p, l, ti))
                for h in range(2):
                    pm = self.pbank("g")
                    for kc in range(8):
                        kb.mm(pm.ap(), merged[:, kc, ti * 128:(ti + 1) * 128], wout[h][:, kc, :], start=(kc == 0), stop=(kc == 7))
                    tmp = self.ring("tmp", tmp_ring)
                    kb.tt(tmp.ap(), pm.ap(), gb[:, 0, h * 512:(h + 1) * 512], ALU.mult)
                    kb.tt(xm[i][:, h * 512:(h + 1) * 512], xt[:, h * 512:(h + 1) * 512], tmp.ap(), ALU.add, eng="pool")
                self.norm_one(xm[i], AB[:, 16:24], AB[:, 24:32], h2T, i * 128)
            for f0 in range(0, DFF, 512):
                fn = min(512, DFF - f0)
                if f0 == 0:
                    wg, wu = pre
                else:
                    wg = next(it); wu = next(it)
                for fc in range(fn // 128):
                    pg = self.pbank("g"); pu = self.pbank("g")
                    self.pfm(pg.ap(), wg, fc * 128, 128, h2T, 8, 0, 512)
                    self.pfm(pu.ap(), wu, fc * 128, 128, h2T, 8, 0, 512)
                    sg = self.ring("sgb", sgb_ring)
                    kb.act(sg.ap(), pg.ap(), AF.Silu)
                    kb.tt(actT[:, f0 // 128 + fc, :], pu.ap(), sg.ap(), ALU.mult)
            for h in range(2):
                for k0 in range(0, 22, 8):
                    kn = min(8, 22 - k0)
                    kb.dma("pool", wd[:, k0:k0 + kn, :], self.d(I["w_down"][l, k0 * 128:(k0 + kn) * 128, h * 512:(h + 1) * 512].rearrange("(kc q) n -> q kc n", q=128)))
                for i in range(4):
                    pd = self.pbank("g")
                    for kc in range(22):
                        kb.mm(pd.ap(), actT[:, kc, i * 128:(i + 1) * 128], wd[:, kc, :], start=(kc == 0), stop=(kc == 21))
                    tmp = self.ring("tmp", tmp_ring)
                    kb.tt(tmp.ap(), pd.ap(), gb[:, 1, h * 512:(h + 1) * 512], ALU.mult)
                    kb.tt(xm[i][:, h * 512:(h + 1) * 512], xm[i][:, h * 512:(h + 1) * 512], tmp.ap(), ALU.add, eng="pool")
            for i in range(4):
                ti = b0 // 128 + i
                if not last:
                    kb.dma("sp", V(self.xT[p.name][ti], self.xbuf[p.name][ti * 128:(ti + 1) * 128, :]), xm[i].ap())
                else:
                    sm = self.ring("sm", self.small)
                    kb.act(self.junk_bf.ap(), xm[i].ap(), AF.Square, accum_out=sm[:, 0:1])
                    kb.act(sm[:, 1:2], sm[:, 0:1], AF.Sqrt, scale=1.0 / D, bias=EPS)
                    kb.recip(sm[:, 2:3], sm[:, 1:2])
                    yo = self.ring("yo", yo_ring)
                    kb.stt(yo.ap(), xm[i].ap(), sm[:, 2:3], self.fnb.ap(), ALU.mult, ALU.mult)
                    kb.dma("sp", self.d(self.O["ys" if p.name == "S" else "yc"][ti * 128:(ti + 1) * 128, :]), yo.ap(), is_output=True)

    def layer(self, p, l):
        kb = self.kb
        mixers = self.cfg.get("mixers", "ABC")
        self.first_mixer = {"A": 0, "B": 1, "C": 2}[mixers[0]]
        with self.phase():
            AB = self.layer_consts(p, l)
            merged = kb.sb("merged", [128, 8, p.T], BF16)
            with self.phase():
                hT = kb.sb("hT", [128, 8, p.T], BF16)
                for i in range(p.T // 128):
                    xt = self.ring("xt", self.xt_ring)
                    kb.dma("sp", xt.ap(), self.xsrc(p, l, i))
                    self.norm_one(xt, AB[:, 0:8], AB[:, 8:16], hT, i * 128)
                self.stage(2)
                if "A" in mixers:
                    with self.phase():
                        self.alloc_wring()
                        yT = self.mixerA(p, l, hT)
                        self.stage(5)
                        self.merge(p, l, hT, yT, merged, "w_pa", 0)
                        self.stage(6)
                if "B" in mixers:
                    with self.phase():
                        self.alloc_wring()
                        yT = self.mixerB(p, l, hT)
                        self.merge(p, l, hT, yT, merged, "w_pb", 1)
                if "C" in mixers:
                    with self.phase():
                        yT = self.mixerC(p, l, hT)
                        with self.phase():
                            self.alloc_wring()
                            self.merge(p, l, hT, yT, merged, "w_pc", 2)
            with self.phase():
                self.alloc_wring()
                self.tail(p, l, merged, AB)

    def build(self):
        with ExitStack() as gst, ExitStack() as st:
            kb = self.kb = KB(self.nc, st, gst)
            self.modT = Tile("modT", None)
            self.qkvTile = {pn: [Tile("qkv%s%d" % (pn, i), None) for i in range(8)] for pn in ("S", "C")}
            self.oTile = {(pn, d): [Tile("o%s%d_%d" % (pn, d, i), None) for i in range(T // 64)]
                          for pn, T in (("S", TS), ("C", TC)) for d in (0, 1)}
            self.xT = {pn: [Tile("x%s%d" % (pn, i), None) for i in range(T // 128)] for pn, T in (("S", TS), ("C", TC))}
            try:
                self.setup()
                self.mods()
                self.stage(1)
                for pn in self.cfg.get("passes", "CS"):
                    p = PassCfg(pn)
                    for l in range(self.cfg.get("layers", L)):
                        self.layer(p, l)
            except _Stop:
                pass
            kb.finish()
            print("ninst", kb.ninst, "nwait", kb.nwait, "nsem", kb.nsem)
        return self.nc


def _core_inputs(core, inp, shared):
    b = core % 4
    m = dict(shared)
    m["xs"] = np.ascontiguousarray(inp["x_sample"][b])
    m["xc"] = np.ascontiguousarray(inp["x_prompt"][2 * core:2 * core + 2].reshape(TC, D))
    m["cka"] = np.ascontiguousarray(inp["cache_ka"][b].reshape(L, PAST, 128))
    m["cva"] = np.ascontiguousarray(inp["cache_va"][b].reshape(L, PAST, 128))
    m["cckv"] = np.ascontiguousarray(inp["cache_ckv"][b])
    m["ckpe"] = np.ascontiguousarray(inp["cache_kpe"][b])
    m["sf"] = np.ascontiguousarray(inp["state_fwd"][b])
    m["sb"] = np.ascontiguousarray(inp["state_bwd"][b])
    m["cv2"] = np.ascontiguousarray(np.stack([inp["c"][b], inp["c_ctx"]], 0))
    return m


def _shared_inputs(inp):
    f = lambda a: np.ascontiguousarray(np.asarray(a, dtype=np.float32))
    sh = {}
    for n in ("w_mod", "b_mod", "norm1", "norm2", "b_qnorm", "b_kvnorm", "w_uq", "c_onorm", "w_pa", "w_pb", "w_pc",
              "w_out", "w_gate", "w_up", "w_down", "final_norm"):
        sh[n] = f(inp[n])
    sh["w_in_r"] = _prep_w_in(f(inp["w_in"]))
    sh["aq"] = f(np.stack([inp["a_qnorm"], inp["a_qnorm"][:, PERM64]], 1))
    sh["akn"] = f(np.stack([inp["a_knorm"], inp["a_knorm"][:, PERM64]], 1))
    wuq = f(inp["w_uq"]).reshape(L, 384, 8, 96)
    wuqs = np.zeros_like(wuq)
    wuqs[..., 64:96] = wuq[..., 64:96][..., PERM32]
    sh["w_uqs"] = np.ascontiguousarray(wuqs.reshape(L, 384, 768))
    wukv = f(inp["w_ukv"]).reshape(L, 256, 8, 128)
    sh["w_ukv_k"] = np.ascontiguousarray(wukv[..., 0:64].reshape(L, 256, 512))
    sh["w_ukv_v"] = np.ascontiguousarray(wukv[..., 64:128].reshape(L, 256, 512))
    sh["c_conv"] = f(inp["c_conv"]).reshape(L, 3, 1536)
    sh["c_alog"] = f(inp["c_alog"]).reshape(L, 16)
    sh["c_dt"] = f(inp["c_dt_bias"]).reshape(L, 16)
    tA, tB = _tables(TS, True)
    sh["tabA_S"], sh["tabB_S"] = tA, tB
    tA, tB = _tables(TC, False)
    sh["tabA_C"], sh["tabB_C"] = tA, tB
    sh["consts"] = _consts()
    return sh


def _run(inp, cfg=None):
    inp = {k: np.asarray(v) for k, v in inp.items()}
    b = Builder(cfg)
    nc = b.build()
    shared = _shared_inputs(inp)
    ncores = (cfg or {}).get("ncores", 8)
    in_maps = [_core_inputs(c, inp, shared) for c in range(ncores)]
    res = run_bass_kernel_spmd(nc, in_maps, core_ids=list(range(ncores)))
    return res.results


def kernel(**inputs):
    r = _run(inputs)
    y_prompt = np.concatenate([r[c]["yc"].reshape(2, 256, D) for c in range(8)], 0)
    y_sample = np.stack([r[c]["ys"] for c in range(4)], 0)
    def cat(n, shp):
        return np.concatenate([r[c][n].reshape((2, L, 256) + shp) for c in range(8)], 0)
    new_ka = cat("nka", (2, 64)); new_va = cat("nva", (2, 64))
    new_ckv = cat("nckv", (256,)); new_kpe = cat("nkpe", (32,))
    new_sf = np.concatenate([r[c]["nsf"] for c in range(8)], 0)
    new_sb = np.concatenate([r[c]["nsb"] for c in range(8)], 0)
    return tuple(np.ascontiguousarray(a.astype(np.float32)) for a in
                 (y_prompt, y_sample, new_ka, new_va, new_ckv, new_kpe, new_sf, new_sb))
```
